# Optimizing a Trainium2 kernel written in Bass

```python
import math
import jax, jax.numpy as jnp
from jax import lax
import numpy as np

D_MODEL = 4096
BATCH = 2
SEQ = 4096
DEPTH = 2

EPS = 1e-6
NEG_BIG = -1e30
BRANCH_WIDTH = D_MODEL // 4
N_BRANCHES = 3

A_DK = 128
A_DV = 128
A_HEADS = BRANCH_WIDTH // A_DK
A_WIDTH = A_HEADS * A_DK
A_CHUNK = 64
A_COLS = 4 * A_WIDTH

B_HD = 128
B_HEADS = BRANCH_WIDTH // B_HD
B_WIDTH = B_HEADS * B_HD
B_PATTERNS = ((128, 1), (512, 4), (2048, 16))
B_GROUPS = len(B_PATTERNS)
B_BLOCK = 128
B_COLS = B_GROUPS * 3 * B_WIDTH
ROPE_THETA = 500000.0
ROT_DIM = B_HD // 4
ROT_HALF = ROT_DIM // 2

C_WIDTH = BRANCH_WIDTH
C_GROUPS = 8
C_GROUP_CH = C_WIDTH // C_GROUPS
C_CHUNK = 128
C_COLS = 2 * C_WIDTH

IN_COLS = A_COLS + B_COLS + C_COLS

D_FF = 3 * D_MODEL
CONV_W = 3

kernel_name = "hybrid_hgrn2_dilated_spatialgate_block"


def rms_norm(x, g):
    x32 = x.astype(jnp.float32)
    return x32 * lax.rsqrt(jnp.mean(x32 * x32, axis=-1, keepdims=True) + EPS) * g.astype(jnp.float32)


def partial_rope(x, positions):
    inv = jnp.power(jnp.float32(ROPE_THETA), -jnp.arange(0, ROT_DIM, 2, dtype=jnp.float32) / ROT_DIM)
    ang = positions.astype(jnp.float32)[..., None] * inv
    cos = jnp.cos(ang)[:, :, None, :]
    sin = jnp.sin(ang)[:, :, None, :]
    x1 = x[..., :ROT_HALF]
    x2 = x[..., ROT_HALF:ROT_DIM]
    return jnp.concatenate([x1 * cos - x2 * sin, x2 * cos + x1 * sin, x[..., ROT_DIM:]], axis=-1)


def hgrn2(q, f_logit, i_in, g, lb, out_gain):
    f32 = jnp.float32
    Bn, S, _ = q.shape
    N = S // A_CHUNK
    lbf = lb.astype(f32)
    f = lbf + (1.0 - lbf) * jax.nn.sigmoid(f_logit.astype(f32))
    log_f = jnp.log(f)
    k = 1.0 - f

    def chunks(t, d):
        return t.astype(f32).reshape(Bn, N, A_CHUNK, A_HEADS, d).transpose(1, 0, 3, 2, 4)

    qc = chunks(q, A_DK) * (A_DK ** -0.5)
    kc = chunks(k, A_DK)
    vc = chunks(i_in, A_DV)
    lfc = chunks(log_f, A_DK)
    causal = jnp.tril(jnp.ones((A_CHUNK, A_CHUNK), dtype=bool))

    def step(state, inp):
        qt, kt, vt, lft = inp
        cum = jnp.cumsum(lft, axis=2)
        o_inter = jnp.einsum('bhtk,bhkv->bhtv', qt * jnp.exp(cum), state)
        rel = jnp.where(causal[None, None, :, :, None],
                        cum[:, :, :, None, :] - cum[:, :, None, :, :], NEG_BIG)
        scores = jnp.einsum('bhtk,bhsk,bhtsk->bhts', qt, kt, jnp.exp(rel))
        o_intra = jnp.einsum('bhts,bhsv->bhtv', scores, vt)
        last = cum[:, :, -1:, :]
        new_state = (jnp.exp(last[:, :, 0, :, None]) * state
                     + jnp.einsum('bhsk,bhsv->bhkv', kt * jnp.exp(last - cum), vt))
        return new_state, o_inter + o_intra

    s0 = jnp.zeros((Bn, A_HEADS, A_DK, A_DV), f32)
    _, o = lax.scan(step, s0, (qc, kc, vc, lfc))
    o = o.transpose(1, 0, 3, 2, 4).reshape(Bn, S, A_HEADS, A_DV)
    o = rms_norm(o, out_gain).reshape(Bn, S, A_HEADS * A_DV) * jax.nn.silu(g.astype(f32))
    return o


def dilated_group(q, k, v, dil, n_back):
    Bn, S, H, hd = q.shape
    L = S // dil
    nb = -(-L // B_BLOCK)
    Lp = nb * B_BLOCK

    def to_res(t):
        t = t.reshape(Bn, L, dil, H, hd).transpose(0, 2, 3, 1, 4)
        t = jnp.pad(t, ((0, 0), (0, 0), (0, 0), (0, Lp - L), (0, 0)))
        return t.reshape(Bn, dil, H, nb, B_BLOCK, hd)

    def with_prev(t):
        prev = jnp.pad(t, ((0, 0), (0, 0), (0, 0), (1, 0), (0, 0), (0, 0)))[:, :, :, :-1]
        return jnp.concatenate([prev, t], axis=4)

    qb = to_res(q)
    kk = with_prev(to_res(k))
    vv = with_prev(to_res(v))
    s = jnp.einsum('brhnqe,brhnke->brhnqk', qb, kk) * (hd ** -0.5)
    qi = jnp.arange(B_BLOCK)[:, None] + B_BLOCK
    ki = jnp.arange(2 * B_BLOCK)[None, :]
    dist = qi - ki
    blk = jnp.arange(nb)[:, None, None]
    valid = (dist >= 0) & (dist <= n_back) & (blk * B_BLOCK + ki - B_BLOCK >= 0)
    s = jnp.where(valid, s, NEG_BIG)
    lse = jax.nn.logsumexp(s, axis=-1)
    p = jnp.exp(s - lse[..., None])
    o = jnp.einsum('brhnqk,brhnke->brhnqe', p, vv)

    def from_res(t):
        tail = t.shape[5:]
        t = t.reshape((Bn, dil, H, Lp) + tail)[:, :, :, :L]
        t = t.transpose((0, 3, 1, 2) + tuple(range(4, t.ndim)))
        return t.reshape((Bn, S, H) + tail)

    return from_res(o), from_res(lse)


def dilated_mixer(proj_b, q_gain, k_gain, positions):
    Bn, S, _ = proj_b.shape
    parts = proj_b.astype(jnp.float32).reshape(Bn, S, B_GROUPS, 3, B_HEADS, B_HD)
    outs = []
    lses = []
    for gi, (win, dil) in enumerate(B_PATTERNS):
        q = partial_rope(rms_norm(parts[:, :, gi, 0], q_gain[gi]), positions)
        k = partial_rope(rms_norm(parts[:, :, gi, 1], k_gain[gi]), positions)
        o, lse = dilated_group(q, k, parts[:, :, gi, 2], dil, win // dil)
        outs.append(o)
        lses.append(lse)
    w = jax.nn.softmax(jnp.stack(lses, axis=0), axis=0)
    o = jnp.sum(w[..., None] * jnp.stack(outs, axis=0), axis=0)
    return o.reshape(Bn, S, B_WIDTH)


def spatial_gating(proj_c, ln_g, ln_b, w_s, b_s):
    f32 = jnp.float32
    Bn, S, _ = proj_c.shape
    z = jax.nn.gelu(proj_c.astype(f32), approximate=False)
    u = z[..., :C_WIDTH]
    v = z[..., C_WIDTH:]
    mu = jnp.mean(v, axis=-1, keepdims=True)
    var = jnp.mean(jnp.square(v - mu), axis=-1, keepdims=True)
    v = (v - mu) * lax.rsqrt(var + EPS) * ln_g.astype(f32) + ln_b.astype(f32)
    N = S // C_CHUNK
    v = v.reshape(Bn, N, C_CHUNK, C_GROUPS, C_GROUP_CH)
    mask = jnp.tril(jnp.ones((C_CHUNK, C_CHUNK), f32))
    mixed = (jnp.einsum('gts,bnsgc->bntgc', w_s.astype(f32) * mask, v)
             + b_s.astype(f32).T[None, None, :, :, None])
    return u * mixed.reshape(Bn, S, C_WIDTH)


def setup_inputs(seed: int = 0) -> dict:
    key = jax.random.key(seed)
    ks = jax.random.split(key, 20)
    f32 = jnp.float32
    nrm = lambda k, shape, scale: jax.random.normal(k, shape, f32) * scale
    return {
        "x": nrm(ks[0], (BATCH, SEQ, D_MODEL), 1.0),
        "positions": jnp.tile(jnp.arange(SEQ, dtype=jnp.int32)[None, :], (BATCH, 1)),
        "norm_mix": 1.0 + nrm(ks[1], (DEPTH, D_MODEL), 0.02),
        "w_in": nrm(ks[2], (DEPTH, D_MODEL, IN_COLS), D_MODEL ** -0.5),
        "hgrn_lower_bounds": nrm(ks[3], (DEPTH, A_WIDTH), 0.5),
        "hgrn_out_norm": 1.0 + nrm(ks[4], (DEPTH, A_DV), 0.02),
        "q_norm": 1.0 + nrm(ks[5], (DEPTH, B_GROUPS, B_HD), 0.02),
        "k_norm": 1.0 + nrm(ks[6], (DEPTH, B_GROUPS, B_HD), 0.02),
        "sg_ln_g": 1.0 + nrm(ks[7], (DEPTH, C_WIDTH), 0.02),
        "sg_ln_b": nrm(ks[8], (DEPTH, C_WIDTH), 0.02),
        "sg_w": nrm(ks[9], (DEPTH, C_GROUPS, C_CHUNK, C_CHUNK), C_CHUNK ** -0.5),
        "sg_b": 1.0 + nrm(ks[10], (DEPTH, C_GROUPS, C_CHUNK), 0.02),
        "w_gate": nrm(ks[11], (DEPTH, D_MODEL, N_BRANCHES * D_MODEL), D_MODEL ** -0.5),
        "w_branch": nrm(ks[12], (DEPTH, N_BRANCHES, BRANCH_WIDTH, D_MODEL), BRANCH_WIDTH ** -0.5),
        "w_out": nrm(ks[13], (DEPTH, D_MODEL, D_MODEL), D_MODEL ** -0.5),
        "norm_ffn": 1.0 + nrm(ks[14], (DEPTH, D_MODEL), 0.02),
        "w_up": nrm(ks[15], (DEPTH, D_MODEL, 2 * D_FF), D_MODEL ** -0.5),
        "ffn_conv_w": nrm(ks[16], (DEPTH, CONV_W, D_FF), CONV_W ** -0.5),
        "ffn_conv_b": nrm(ks[17], (DEPTH, D_FF), 0.02),
        "w_down": nrm(ks[18], (DEPTH, D_FF, D_MODEL), D_FF ** -0.5),
    }


def reference(x, positions, norm_mix, w_in, hgrn_lower_bounds, hgrn_out_norm, q_norm, k_norm,
              sg_ln_g, sg_ln_b, sg_w, sg_b, w_gate, w_branch, w_out, norm_ffn, w_up,
              ffn_conv_w, ffn_conv_b, w_down):
    Bn, S, D = x.shape
    sm = jax.nn.softmax(hgrn_lower_bounds.astype(jnp.float32), axis=0)
    lower_bounds = jnp.cumsum(sm, axis=0) - sm[0:1]
    for l in range(DEPTH):
        xn = rms_norm(x, norm_mix[l]).astype(x.dtype)
        proj = xn @ w_in[l]
        pa = proj[..., :A_COLS]
        pb = proj[..., A_COLS:A_COLS + B_COLS]
        pc = proj[..., A_COLS + B_COLS:]
        qa, fa, ia, ga = jnp.split(pa, 4, axis=-1)
        ya = hgrn2(qa, fa, ia, ga, lower_bounds[l], hgrn_out_norm[l])
        yb = dilated_mixer(pb, q_norm[l], k_norm[l], positions)
        yc = spatial_gating(pc, sg_ln_g[l], sg_ln_b[l], sg_w[l], sg_b[l])
        ys = jnp.stack([ya, yb, yc], axis=2).astype(x.dtype)
        branches = jnp.einsum('bsiw,iwd->bsid', ys, w_branch[l])
        gates = jax.nn.sigmoid((xn @ w_gate[l]).reshape(Bn, S, N_BRANCHES, D))
        merged = jnp.sum(gates * branches, axis=2)
        x = x + merged @ w_out[l]
        xn = rms_norm(x, norm_ffn[l]).astype(x.dtype)
        up = xn @ w_up[l]
        gate_h = up[..., :D_FF]
        val_h = up[..., D_FF:]
        gp = jnp.pad(gate_h, ((0, 0), (CONV_W - 1, 0), (0, 0)))
        cw = ffn_conv_w[l]
        conv = (cw[0] * gp[:, 0:S] + cw[1] * gp[:, 1:S + 1] + cw[2] * gp[:, 2:S + 2]
                + ffn_conv_b[l])
        x = x + (jax.nn.silu(conv) * val_h) @ w_down[l]
    return x
```

```python
import math
import numpy as np
import concourse.bass as bass
import concourse.mybir as mybir
from concourse.bass_utils import run_bass_kernel_spmd

F32 = mybir.dt.float32
BF16 = mybir.dt.bfloat16
I32 = mybir.dt.int32
AF = mybir.ActivationFunctionType
ALU = mybir.AluOpType
P = 128
T = 512
EPS = 1e-6
ROPE_THETA = 500000.0
DILS = (1, 4, 16)
SAME_ENGINE_SYNC = True
MASKENG = "pool"


class _Op:
    __slots__ = ("eng", "fn", "deps", "key", "val", "is_dma", "needs_inc", "idx", "desc")


class Tracker:
    ENG = ("pe", "act", "dve", "pool", "sp")

    def __init__(self):
        self.ops = {e: [] for e in self.ENG}
        self.lastw = {}
        self.readers = {}
        self.dma_cnt = {}
        self.last_dma = {}

    def add(self, eng, fn, r=(), w=(), dma_key=None):
        op = _Op()
        op.eng = eng
        op.fn = fn
        op.is_dma = dma_key is not None
        op.key = dma_key if op.is_dma else eng
        op.needs_inc = op.is_dma
        op.val = 0
        deps = {}

        def dep(o):
            if o is None or o is op:
                return
            if o.is_dma:
                o = self.last_dma[o.key]
                if o is op:
                    return
            k = o.key
            if k not in deps or deps[k].idx < o.idx:
                deps[k] = o

        for k in r:
            dep(self.lastw.get(k))
        for k in w:
            dep(self.lastw.get(k))
            for o in self.readers.get(k, {}).values():
                dep(o)
        if op.is_dma:
            n = self.dma_cnt.get(dma_key, 0) + 1
            self.dma_cnt[dma_key] = n
            op.idx = n
            op.val = 16 * n
            self.last_dma[dma_key] = op
        else:
            op.idx = len(self.ops[eng])
        op.deps = list(deps.values())
        op.desc = (list(r), list(w))
        for k in r:
            self.readers.setdefault(k, {})[op.key] = op
        for k in w:
            self.lastw[k] = op
            self.readers[k] = {}
        self.ops[eng].append(op)
        return op

    def barrier(self):
        lasts = []
        for e in self.ENG:
            for o in reversed(self.ops[e]):
                if not o.is_dma:
                    lasts.append(o)
                    break
        lasts += list(self.last_dma.values())
        for e in self.ENG:
            op = _Op()
            op.eng = e
            op.fn = None
            op.is_dma = False
            op.key = e
            op.needs_inc = False
            op.val = 0
            op.idx = len(self.ops[e])
            op.deps = [o for o in lasts if not (o.key == e)]
            op.desc = ('barrier',)
            self.ops[e].append(op)
        self.lastw = {}
        self.readers = {}

    def finalize(self):
        for e in self.ENG:
            for op in self.ops[e]:
                for d in op.deps:
                    if not d.is_dma and (d.eng != op.eng or op.is_dma or (SAME_ENGINE_SYNC and d.eng != "pe")):
                        d.needs_inc = True
        for e in self.ENG:
            c = 0
            for op in self.ops[e]:
                if not op.is_dma and op.needs_inc:
                    if op.fn is None:
                        op.needs_inc = False
                        continue
                    c += 1
                    op.val = c


    def simulate(self):
        sem = {}
        pc = {e: 0 for e in self.ENG}
        progress = True
        while progress:
            progress = False
            for e in self.ENG:
                while pc[e] < len(self.ops[e]):
                    op = self.ops[e][pc[e]]
                    ok = True
                    for d in op.deps:
                        if not d.is_dma and d.eng == op.eng and not op.is_dma and not (SAME_ENGINE_SYNC and d.eng != "pe"):
                            continue
                        if d.val <= 0:
                            continue
                        if sem.get(d.key, 0) < d.val:
                            ok = False
                            break
                    if not ok:
                        break
                    if op.fn is not None:
                        if op.is_dma:
                            sem[op.key] = sem.get(op.key, 0) + 16
                        elif op.needs_inc:
                            sem[op.key] = sem.get(op.key, 0) + 1
                    pc[e] += 1
                    progress = True
        stuck = {e: (pc[e], len(self.ops[e])) for e in self.ENG if pc[e] < len(self.ops[e])}
        if stuck:
            msg = []
            for e, (i, n) in stuck.items():
                op = self.ops[e][i]
                msg.append((e, i, n, [(d.key, d.val, sem.get(d.key, 0)) for d in op.deps if d.val > 0 and sem.get(d.key, 0) < d.val]))
            raise RuntimeError(f"tracker deadlock: {msg}")


    def dump_tail(self, n=14):
        for e in self.ENG:
            print("==== engine", e, len(self.ops[e]))
            waited = {}
            rows = []
            for op in self.ops[e]:
                ws = []
                for d in op.deps:
                    if not d.is_dma and d.eng == op.eng and not op.is_dma and not (SAME_ENGINE_SYNC and d.eng != "pe"):
                        continue
                    if d.val <= 0 or waited.get(d.key, 0) >= d.val:
                        continue
                    waited[d.key] = d.val
                    ws.append((d.key, d.val))
                inc = (op.key, 16 if op.is_dma else 1, op.val) if (op.fn is not None and (op.is_dma or op.needs_inc)) else None
                rows.append((ws, inc, op.desc))
            for r_ in rows[-n:]:
                print("   wait", r_[0], "| inc", r_[1], "|", r_[2])

    def emit(self, eng, e, sems):
        waited = {}
        for op in self.ops[eng]:
            for d in op.deps:
                if not d.is_dma and d.eng == op.eng and not op.is_dma and not (SAME_ENGINE_SYNC and d.eng != "pe"):
                    continue
                if d.val <= 0:
                    continue
                if waited.get(d.key, 0) >= d.val:
                    continue
                waited[d.key] = d.val
                e.wait_ge(sems[d.key], d.val)
            if op.fn is None:
                continue
            ins = op.fn(e)
            if op.is_dma:
                ins.then_inc(sems[op.key], 16)
            elif op.needs_inc:
                ins.then_inc(sems[op.key], 1)
        if eng == "sp":
            for k, o in self.last_dma.items():
                if waited.get(k, 0) < o.val:
                    e.wait_ge(sems[k], o.val)


def _tile_cols(W):
    K, N = W.shape
    return np.ascontiguousarray(W.reshape(K // P, P, N // P, P).transpose(2, 1, 0, 3))


def _dims(D, S):
    KC = D // P
    BW = D // 4
    NB = BW // P
    DFF = 3 * D
    FC = DFF // P
    NTB = S // T
    NT = NB * 15 + KC + KC * 3 + KC + 3 * (3 * KC)
    return KC, BW, NB, DFF, FC, NTB, NT


def _in_cols(BW, h):
    A = 4 * BW
    Bc = 9 * BW
    cols = [1 * BW, 0 * BW, 2 * BW, 3 * BW]
    for gi in range(3):
        for j in range(3):
            cols.append(A + gi * 3 * BW + j * BW)
    cols += [A + Bc, A + Bc + BW]
    return [c + h * P for c in cols]


def pack_weights(D, S, l, w_in, w_gate, w_branch, w_out, w_up, w_down):
    KC, BW, NB, DFF, FC, NTB, NT = _dims(D, S)
    WT = KC * P
    out = np.empty((NT, P, WT), np.float32)
    t = 0
    win_t = _tile_cols(w_in[l])
    for h in range(NB):
        for c in _in_cols(BW, h):
            out[t] = win_t[c // P].reshape(P, WT)
            t += 1
    wg_t = _tile_cols(w_gate[l])
    wb_t = [_tile_cols(w_branch[l, i]) for i in range(3)]
    for j in range(KC):
        out[t] = 0.0
        out[t].reshape(P, -1)[:, :3 * NB * P] = np.stack([wb_t[i][j] for i in range(3)], axis=1).reshape(P, 3 * NB * P)
        t += 1
        for i in range(3):
            out[t] = wg_t[i * KC + j].reshape(P, WT)
            t += 1
    wo_t = _tile_cols(w_out[l])
    for j in range(KC):
        out[t] = wo_t[j].reshape(P, WT)
        t += 1
    wu_t = _tile_cols(w_up[l])
    for pt in range(3):
        for jb in range(KC):
            out[t] = wu_t[pt * KC + jb].reshape(P, WT)
            t += 1
            out[t] = wu_t[FC + pt * KC + jb].reshape(P, WT)
            t += 1
        wd_t = _tile_cols(w_down[l][pt * D:(pt + 1) * D])
        for j in range(KC):
            out[t] = wd_t[j].reshape(P, WT)
            t += 1
    assert t == NT
    return out


def make_consts_full():
    c = np.zeros((P, 10 * P), np.float32)
    i = np.arange(P)
    c[:, 0:P] = np.eye(P)
    c[:, P:2 * P] = (i[:, None] <= i[None, :])
    c[:, 2 * P:3 * P] = (i[:, None] >= i[None, :])
    R = np.zeros((P, P), np.float32)
    for e in range(16):
        R[e + 16, e] = -1.0
        R[e, e + 16] = 1.0
    c[:, 3 * P:4 * P] = R
    c[:, 4 * P:5 * P] = 1.0
    m = np.ones(T, np.float32)
    m[::64] = 0.0
    c[:, 5 * P:5 * P + T] = m[None, :]
    inv = np.zeros(P, np.float32)
    j = np.arange(16, dtype=np.float32)
    invf = np.power(np.float32(ROPE_THETA), -(2 * j) / np.float32(32)).astype(np.float32)
    inv[0:16] = invf
    inv[16:32] = invf
    c[:, 9 * P] = inv
    return c


def pack_cols(D, S, l, norm_mix, hgrn_out_norm, q_norm, k_norm, sg_ln_g, sg_ln_b, norm_ffn, ffn_conv_w, ffn_conv_b):
    KC, BW, NB, DFF, FC, NTB, NT = _dims(D, S)
    parts = [norm_mix[l].reshape(KC, P).T, norm_ffn[l].reshape(KC, P).T, hgrn_out_norm[l].reshape(1, P).T,
             q_norm[l].T, k_norm[l].T, sg_ln_g[l].reshape(NB, P).T, sg_ln_b[l].reshape(NB, P).T,
             ffn_conv_w[l, 0].reshape(FC, P).T, ffn_conv_w[l, 1].reshape(FC, P).T, ffn_conv_w[l, 2].reshape(FC, P).T,
             ffn_conv_b[l].reshape(FC, P).T]
    return np.ascontiguousarray(np.concatenate(parts, axis=1).astype(np.float32))


STOP = [0]
DEBUG = [0]
LAST = {}


class _Stop(Exception):
    pass


CUR = [0]


def ck(n):
    if STOP[0] == n + 100 * CUR[0]:
        raise _Stop()


def build(D, S, NL):
    KC, BW, NB, DFF, FC, NTB, NT = _dims(D, S)
    WT = KC * P
    NCP = 2 * KC + 1 + 6 + 2 * NB + 4 * FC
    nc = bass.Bass("TRN2", target_bir_lowering=False)
    try:
        nc.allow_low_precision("bf16 matmul operands with fp32 accumulation (reference bar is bf16-level)")
    except Exception:
        pass
    xT = nc.dram_tensor("xT", [D, S], F32, kind="ExternalInput").ap()
    posd = nc.dram_tensor("pos", [P, S], I32, kind="ExternalInput").ap()
    constd = nc.dram_tensor("consts", [P, 10 * P], F32, kind="ExternalInput").ap()
    colsd = nc.dram_tensor("cols", [NL * P, NCP], F32, kind="ExternalInput").ap()
    LBW = max(NB, 8)
    lbd = nc.dram_tensor("lbraw", [P, NL * LBW], F32, kind="ExternalInput").ap()
    sgbd = nc.dram_tensor("sgb", [NL * P, NB * P], F32, kind="ExternalInput").ap()
    sgwd = nc.dram_tensor("sgwT", [NL * P, NB * P], F32, kind="ExternalInput").ap()
    wall = nc.dram_tensor("wall", [NL * NT * P, WT], F32, kind="ExternalInput").ap()
    outT = nc.dram_tensor("outT", [D, S], F32, kind="ExternalOutput").ap()
    dbg = nc.dram_tensor("dbg", [3 * NB * P, S], BF16, kind="ExternalOutput").ap() if DEBUG[0] else None
    dbg2 = nc.dram_tensor("dbg2", [P, 1024], F32, kind="ExternalOutput").ap() if DEBUG[0] else None
    TPC = 128
    NCH = (NL * NT + TPC - 1) // TPC
    w16c = [nc.dram_tensor(f"w16_{c}", [min(TPC, NL * NT - c * TPC) * P, WT], BF16).ap() for c in range(NCH)]
    xmid = nc.dram_tensor("xmid", [D, S], F32).ap()
    ksc = [nc.dram_tensor(f"ksc{g}", [NB * P, S], BF16).ap() for g in range(3)]
    vsc = [nc.dram_tensor(f"vsc{g}", [NB * S, P], BF16).ap() for g in range(3)]

    tr = Tracker()
    import contextlib
    es = contextlib.ExitStack()

    def sb(name, shape, dt):
        return es.enter_context(nc.sbuf_tensor(name, shape, dt))

    def ps(name, shape, dt):
        return es.enter_context(nc.psum_tensor(name, shape, dt))

    with es:
        NSLOT = 2

        class Slots:
            def __init__(self, aps):
                self.aps = aps

            def __getitem__(self, k):
                p_, i_, f_ = k
                return self.aps[i_][p_, f_]

        TW = 5 * T + 8 * T // 2 + 2 * NB * T // 2 + 3 * 16 * P // 2 + 2 * 8 * P // 2 + 8 * P // 2 + 2 * 64 // 2 + 2 * T + NB * P + 16 * P // 2
        AW = max(KC * T, TW)
        arena = sb("arena", [P, AW], F32)
        xres = arena[:, 0:KC * T].rearrange("p (k t) -> p k t", t=T)
        _o = [0]

        def carve(words):
            a0 = _o[0]
            _o[0] += words
            assert _o[0] <= AW
            return arena[:, a0:a0 + words]

        tfa = sb("tfa", [P, 5, T], F32)
        tf = Slots([tfa[:, i, :] for i in range(5)] + [carve(T) for _ in range(5)])
        th = Slots([carve(T // 2).bitcast(BF16) for _ in range(8)])
        uz = carve(NB * T // 2).bitcast(BF16).rearrange("p (n t) -> p n t", t=T)
        vz = carve(NB * T // 2).bitcast(BF16).rearrange("p (n t) -> p n t", t=T)
        avt = carve(16 * P // 2).bitcast(BF16).rearrange("p (n t) -> p n t", t=P)
        avp = carve(16 * P // 2).bitcast(BF16).rearrange("p (n t) -> p n t", t=P)
        akp = carve(16 * P // 2).bitcast(BF16).rearrange("p (n t) -> p n t", t=P)
        ktm = carve(8 * P // 2).bitcast(BF16).rearrange("p (n t) -> p n t", t=P)
        vtm = carve(8 * P // 2).bitcast(BF16).rearrange("p (n t) -> p n t", t=P)
        pt_ = carve(4 * P // 2).bitcast(BF16).rearrange("p (a b t) -> p a b t", a=2, b=2)
        atm = carve(64).bitcast(BF16).rearrange("p (a t) -> p a t", a=2)
        Ct = carve(T)
        Snt = carve(T)
        sgwf = carve(NB * P)
        kst = carve(16 * P // 2).bitcast(BF16)
        xn = sb("xn", [P, KC, T], BF16)
        ys = sb("ys", [P, 3 * NB, T], BF16)
        mh = sb("mh", [P, KC, T], BF16)
        wsl = sb("wsl", [P, NSLOT, WT], BF16)
        cst = sb("cst", [P, 10 * P], F32)
        cstb = sb("cstb", [P, 5 * P], BF16)
        cols = sb("cols_s", [P, NL, NCP], F32)
        lbr = sb("lbr_s", [P, NL * LBW], F32)
        lbc = sb("lbc", [P, 2 * LBW], F32)
        sgb = sb("sgb_s", [P, NB * P], F32)
        sgw = sb("sgw", [P, NB * P], BF16)
        rgq = sb("rgq", [P, 3, P], BF16)
        rgk = sb("rgk", [P, 3, P], BF16)
        Sst = sb("Sst", [P, NB, P], F32)
        Sbf = sb("Sbf", [P, NB, P], BF16)
        carry = sb("carry", [P, FC, 2], F32)
        gext = sb("gext", [P, T + 2], F32)
        dch = sb("dch", [P, 8], F32)
        bl = sb("bl", [P, 8], F32)
        posi = tfa[:, 4, :].bitcast(I32)
        epsc = sb("epsc", [P, 1], F32)
        sflag = sb("sflag", [P, T], F32)
        LAST['sbuf_free'] = nc.sbuf_bytes_remaining
        pbank = [ps(f"pb{i}", [P, T], F32) for i in range(7)]
        ptr = ps("ptr", [P, 2 * T], BF16)

        ident_b = cstb[:, 0:P]
        ucur_b = cstb[:, P:2 * P]
        uprev_b = cstb[:, 2 * P:3 * P]
        ones_b = cstb[:, 4 * P:5 * P]
        ones_f = cst[:, 4 * P:5 * P]
        ucur_f = cst[:, P:2 * P]
        smask = cst[:, 5 * P:5 * P + T]
        invcol = cst[:, 9 * P:9 * P + 1]

        def dma(q, out, in_, r, w, key):
            tr.add(q, lambda e: e.dma_start(out=out, in_=in_), r=r, w=w, dma_key=key)

        def mm(out, lhsT, rhs, start, stop, r, w):
            tr.add("pe", lambda e: e.matmul(out, lhsT, rhs, start=start, stop=stop), r=r, w=w)

        def tp(out, in_, r, w):
            tr.add("pe", lambda e: e.transpose(out, in_, ident_b[:in_.shape[0], :in_.shape[0]]), r=r, w=w)

        def act(out, in_, func, r, w, scale=1.0, bias=0.0):
            tr.add("act", lambda e: e.activation(out, in_, func, bias=bias, scale=scale), r=r, w=w)

        def tt(eng, out, a, b, op, r, w):
            tr.add(eng, lambda e: e.tensor_tensor(out, a, b, op), r=r, w=w)

        def tsc(eng, out, a, s1, s2, op0, op1, r, w):
            if s2 is None:
                tr.add(eng, lambda e: e.tensor_scalar(out, a, s1, None, op0), r=r, w=w)
            else:
                tr.add(eng, lambda e: e.tensor_scalar(out, a, s1, s2, op0, op1), r=r, w=w)

        def stt(eng, out, a, s, b, op0, op1, r, w):
            tr.add(eng, lambda e: e.scalar_tensor_tensor(out, a, s, b, op0, op1), r=r, w=w)

        def cp(eng, out, in_, r, w):
            if eng == "act":
                tr.add(eng, lambda e: e.copy(out, in_), r=r, w=w)
            else:
                tr.add(eng, lambda e: e.tensor_copy(out, in_), r=r, w=w)

        dma("sp", cst[:, :], constd[:, :], [], ["cst"], "k_cst")
        dma("sp", cols[:, :, :], colsd.rearrange("(l p) n -> p l n", p=P), [], ["cols"], "k_cst")
        dma("sp", lbr[:, :], lbd[:, :], [], ["lbr"], "k_cst")
        cp("dve", cstb[:, :], cst[:, 0:5 * P], ["cst"], ["cstb"])
        tr.add("dve", lambda e: e.memset(epsc[:, :], EPS), r=[], w=["epsc"])
        tsc("dve", sflag[:, :], cst[:, 5 * P:5 * P + T], -1.0, 1.0, ALU.mult, ALU.add, ["cst"], ["sflag"])
        CH = 8
        NG = 4
        GT = ((NT + NG - 1) // NG + CH - 1) // CH * CH

        def cast_chunks(l):
            out, ti0 = [], 0
            while ti0 < NT:
                gt = l * NT + ti0
                c_, o_ = gt // TPC, gt % TPC
                n = min(CH, NT - ti0, TPC - o_, GT - (ti0 % GT))
                out.append((l, ti0, n, c_, o_))
                ti0 += n
            return out

        def issue_cast(ch):
            l_, ti0, n, c_, o_ = ch
            gt = l_ * NT + ti0
            dma("pool", w16c[c_][o_ * P:(o_ + n) * P, :], wall[gt * P:(gt + n) * P, :], [], [("w16", l_, ti0 // GT)],
                f"k_c{l_}_{ti0 // GT}")

        for ch in cast_chunks(0):
            issue_cast(ch)
        pending_cast = {l_: cast_chunks(l_) for l_ in range(1, NL)}

        wctr = [0]

        def wload(l):
            i = wctr[0]
            wctr[0] += 1
            slot = i % NSLOT
            gi_ = l * NT + (i % NT)
            row = (gi_ % TPC) * P
            dma("sp", wsl[:, slot, :], w16c[gi_ // TPC][row:row + P, :], [("w16", l, (i % NT) // GT)], [("wsl", slot)], f"k_w{slot}")
            return slot

        pbi = [0]

        def next_pb():
            b = pbi[0] % 2
            pbi[0] += 1
            return b

        def proj(l, rhs_of_kc, nk, rkeys, sub=None):
            slot = wload(l)
            b = next_pb()
            for kc in range(nk):
                mm(pbank[b][:, :], wsl[:, slot, kc * P:(kc + 1) * P], rhs_of_kc(kc), kc == 0, kc == nk - 1,
                   [("wsl", slot)] + rkeys, [("pb", b)])
            return b

        def rstd_from(bank_key, bank_ap, n, out_ap, okey):
            act(out_ap, bank_ap, AF.Sqrt, [bank_key, "epsc"], [okey], scale=1.0 / n, bias=epsc[:, 0:1])
            tr.add("dve", lambda e: e.reciprocal(out_ap, out_ap), r=[okey], w=[okey])

        def norm_block(l, gofs):
            for kc in range(KC):
                s = kc % 2
                act(tf[:, s, :], xres[:, kc, :], AF.Square, ["xres"], [("tf", s)])
                mm(pbank[2][:, :], ones_f, tf[:, s, :], kc == 0, kc == KC - 1, [("tf", s), "cst"], [("pb", 2)])
            rstd_from(("pb", 2), pbank[2][:, :], D, tf[:, 2, :], ("tf", 2))
            for kc in range(KC):
                stt("dve", xn[:, kc, :], xres[:, kc, :], cols[:, l, gofs + kc:gofs + kc + 1], tf[:, 2, :],
                    ALU.mult, ALU.mult, ["xres", ("tf", 2), "cols"], ["xn"])

        O_GM, O_GF, O_HG = 0, KC, 2 * KC
        O_QN, O_KN = 2 * KC + 1, 2 * KC + 4
        O_LG, O_LB = 2 * KC + 7, 2 * KC + 7 + NB
        O_CW, O_CB = 2 * KC + 7 + 2 * NB, 2 * KC + 7 + 2 * NB + 3 * FC

        try:
            for l in range(NL):
                src = xT if l == 0 else xmid
                dst = xmid if l < NL - 1 else outT
                tr.barrier()
                if l == 0:
                    tr.add("dve", lambda e: e.memset(lbc[:, 0:LBW], 0.0), r=[], w=["lbc"])
                    tr.add("dve", lambda e: e.memset(lbc[:, LBW:2 * LBW], 1.0), r=[], w=["lbc"])
                else:
                    assert l == 1 and NL == 2
                    tt("dve", lbc[:, 0:LBW], lbr[:, LBW:2 * LBW], lbr[:, 0:LBW], ALU.subtract, ["lbr"], ["lbc"])
                    act(lbc[:, 0:LBW], lbc[:, 0:LBW], AF.Sigmoid, ["lbc"], ["lbc"])
                    tsc("dve", lbc[:, LBW:2 * LBW], lbc[:, 0:LBW], -1.0, 1.0, ALU.mult, ALU.add, ["lbc"], ["lbc"])
                ck(90)
                tr.add("dve", lambda e: e.memset(Sst[:, :, :], 0.0), r=[], w=["Sst"])
                tr.add("dve", lambda e: e.memset(Sbf[:, :, :], 0.0), r=[], w=["Sbf"])
                tr.add("dve", lambda e: e.memset(carry[:, :, :], 0.0), r=[], w=["carry"])
                ck(91)
                dma("sp", sgb[:, :], sgbd[l * P:(l + 1) * P, :], [], ["sgb"], "k_cst")
                dma("sp", sgwf[:, :], sgwd[l * P:(l + 1) * P, :], [], ["sgwf"], "k_cst")
                ck(92)
                for g in range(NB):
                    tt("dve", sgw[:, g * P:(g + 1) * P], sgwf[:, g * P:(g + 1) * P], ucur_f, ALU.mult,
                       ["sgwf", "cst"], ["sgw"])
                ck(93)
                for gi in range(3):
                    tsc("dve", rgq[:, gi, :], cst[:, 3 * P:4 * P], cols[:, l, O_QN + gi:O_QN + gi + 1], None, ALU.mult, None,
                        ["cst", "cols"], ["rg"])
                    tsc("dve", rgk[:, gi, :], cst[:, 3 * P:4 * P], cols[:, l, O_KN + gi:O_KN + gi + 1], None, ALU.mult, None,
                        ["cst", "cols"], ["rg"])
                tr.barrier()

                ck(2)
                for tb in range(NTB):
                    t0 = tb * T
                    CUR[0] = tb + NTB * l
                    for k0 in range(0, KC, 4):
                        dma("pool", xres[:, k0:k0 + 4, :], src.rearrange("(kc p) t -> p kc t", p=P)[:, k0:k0 + 4, t0:t0 + T],
                            [("xd", l)], ["xres"], "k_x")
                    norm_block(l, O_GM)
                    tr.barrier()
                    ck(3)
                    dma("pool", posi[:, :], posd[:, t0:t0 + T], [], [("tf", 4)], "k_pos")
                    cp("dve", tf[:, 3, :], posi[:, :], [("tf", 4)], [("tf", 3)])
                    tsc("dve", tf[:, 3, :], tf[:, 3, :], invcol, None, ALU.mult, None, [("tf", 3), "cst"], [("tf", 3)])
                    for (shift, dstt, dkey) in ((0.0, Snt, "Snt"), (0.5 * math.pi, Ct, "Ct")):
                        tsc("dve", tf[:, 0, :], tf[:, 3, :], shift, None, ALU.add, None, [("tf", 3)], [("tf", 0)])
                        tsc("dve", tf[:, 1, :], tf[:, 0, :], 1.0 / (2 * math.pi), None, ALU.mult, None, [("tf", 0)], [("tf", 1)])
                        cp("dve", tf[:, 4, :].bitcast(I32), tf[:, 1, :], [("tf", 1)], [("tf", 4)])
                        cp("dve", tf[:, 1, :], tf[:, 4, :].bitcast(I32), [("tf", 4)], [("tf", 1)])
                        stt("dve", tf[:, 0, :], tf[:, 1, :], -2 * math.pi, tf[:, 0, :], ALU.mult, ALU.add, [("tf", 1), ("tf", 0)], [("tf", 0)])
                        tsc("dve", tf[:, 1, :], tf[:, 0, :], math.pi, -2 * math.pi, ALU.is_gt, ALU.mult, [("tf", 0)], [("tf", 1)])
                        tt("dve", tf[:, 0, :], tf[:, 0, :], tf[:, 1, :], ALU.add, [("tf", 0), ("tf", 1)], [("tf", 0)])
                        act(dstt[:, :], tf[:, 0, :], AF.Sin, [("tf", 0)], [dkey])

                    ck(4)
                    xnk = lambda kc: xn[:, kc, :]
                    for h in range(NB):
                        b = proj(l, xnk, KC, ["xn"])
                        act(tf[:, 0, :], pbank[b][:, :], AF.Sigmoid, [("pb", b)], [("tf", 0)])
                        tsc("dve", tf[:, 0, :], tf[:, 0, :], lbc[:, LBW + h:LBW + h + 1], lbc[:, h:h + 1], ALU.mult, ALU.add,
                            [("tf", 0), "lbc"], [("tf", 0)])
                        tr.add("dve", lambda e: e.tensor_tensor_scan(tf[:, 3, :], sflag[:, :], tf[:, 0, :], 1.0, ALU.max, ALU.mult),
                               r=[("tf", 0), "sflag"], w=[("tf", 3)])
                        tsc("dve", tf[:, 0, :], tf[:, 0, :], -1.0, 1.0, ALU.mult, ALU.add, [("tf", 0)], [("tf", 0)])
                        tr.add("dve", lambda e: e.reciprocal(tf[:, 4, :], tf[:, 3, :]), r=[("tf", 3)], w=[("tf", 4)])
                        tt("dve", tf[:, 1, :], tf[:, 0, :], tf[:, 4, :], ALU.mult, [("tf", 0), ("tf", 4)], [("tf", 1)])
                        cp("dve", th[:, 0, :], tf[:, 1, :], [("tf", 1)], [("th", 0)])
                        cp("act", dch[:, :], tf[:, 3, :].rearrange("p (c s) -> p c s", s=64)[:, :, 63], [("tf", 3)], ["dch"])
                        tt("dve", th[:, 1, :].rearrange("p (c s) -> p c s", s=64), tf[:, 1, :].rearrange("p (c s) -> p c s", s=64),
                           dch[:, :].unsqueeze(2).to_broadcast([P, 8, 64]), ALU.mult, [("tf", 1), "dch"], [("th", 1)])
                        for c in range(8):
                            tp(ptr[:64, c * P:(c + 1) * P], th[:, 1, c * 64:(c + 1) * 64], [("th", 1), "cstb"], ["ptr"])
                        cp("act", ktm[:64, :, :], ptr[:64, 0:8 * P].rearrange("p (c k) -> p c k", k=P), ["ptr"], ["ktm"])
                        b = proj(l, xnk, KC, ["xn"])
                        stt("dve", th[:, 2, :], pbank[b][:, :], float(P) ** -0.5, tf[:, 3, :], ALU.mult, ALU.mult,
                            [("pb", b), ("tf", 3)], [("th", 2)])
                        b = proj(l, xnk, KC, ["xn"])
                        cp("act", th[:, 3, :], pbank[b][:, :], [("pb", b)], [("th", 3)])
                        for c in range(8):
                            tp(ptr[:64, c * P:(c + 1) * P], th[:, 3, c * 64:(c + 1) * 64], [("th", 3), "cstb"], ["ptr"])
                        cp("act", vtm[:64, :, :], ptr[:64, 0:8 * P].rearrange("p (c k) -> p c k", k=P), ["ptr"], ["vtm"])
                        b = proj(l, xnk, KC, ["xn"])
                        act(tf[:, 5, :], pbank[b][:, :], AF.Silu, [("pb", b)], [("tf", 5)])
                        for c in range(8):
                            cs = slice(c * 64, (c + 1) * 64)
                            a = c % 2
                            ab = 3 if a == 0 else 6
                            mm(pbank[ab][:64, 0:64], th[:, 0, cs], th[:, 2, cs], True, True,
                               [("th", 0), ("th", 2)], [("pb", ab)])
                            tt("dve", atm[:64, a, :], pbank[ab][:64, 0:64], ucur_f[:64, :64], ALU.mult,
                               [("pb", ab), "cst"], [("atm", a)])
                            mm(pbank[5][:, cs], Sbf[:, h, :], th[:, 2, cs], True, False, [("Sbf", h), ("th", 2)], [("pb", 5)])
                            mm(pbank[5][:, cs], vtm[:64, c, :], atm[:64, a, :], False, True, ["vtm", ("atm", a)], [("pb", 5)])
                            mm(pbank[4][:, 0:P], ktm[:64, c, :], vtm[:64, c, :], True, True, ["ktm", "vtm"], [("pb", 4)])
                            stt("dve", Sst[:, h, :], Sst[:, h, :], dch[:, c:c + 1], pbank[4][:, 0:P], ALU.mult, ALU.add,
                                [("pb", 4), "dch", ("Sst", h)], [("Sst", h)])
                            cp("act", Sbf[:, h, :], Sst[:, h, :], [("Sst", h)], [("Sbf", h)])
                        act(th[:, 4, :], pbank[5][:, :], AF.Square, [("pb", 5)], [("th", 4)])
                        mm(pbank[2][:, :], ones_b, th[:, 4, :], True, True, [("th", 4), "cstb"], [("pb", 2)])
                        rstd_from(("pb", 2), pbank[2][:, :], P, tf[:, 6, :], ("tf", 6))
                        tt("dve", tf[:, 7, :], pbank[5][:, :], tf[:, 6, :], ALU.mult, [("pb", 5), ("tf", 6)], [("tf", 7)])
                        stt("dve", ys[:, h, :], tf[:, 7, :], cols[:, l, O_HG:O_HG + 1], tf[:, 5, :], ALU.mult, ALU.mult,
                            [("tf", 7), ("tf", 5), "cols"], ["ys"])

                        if DEBUG[0] and l == 0 and tb == 0 and h == 0:
                            dma("pool", dbg2[:, 0:P], Sst[:, 0, :], [("Sst", 0)], [], "k_dbg")
                            dma("pool", dbg2[:, P:P + 8], dch[:, :], ["dch"], [], "k_dbg")
                            dma("pool", dbg2[:, 2 * P:2 * P + T], tf[:, 3, :], [("tf", 3)], [], "k_dbg")
                        ck(5)
                        for gi, d in enumerate(DILS):
                            nq = T // d if d > 1 else P
                            ntl = max(d, 4)
                            L = S // d
                            m0 = t0 // d
                            npv = min(P, m0) if d > 1 else (P if tb > 0 else 0)
                            if npv > 0:
                                if d == 1:
                                    dma("pool", akp[:, 0, :], ksc[gi][h * P:(h + 1) * P, t0 - P:t0],
                                        [("ksc", gi, h)], ["akp"], f"k_kp{gi}")
                                    dma("pool", avp[:, 0, :], vsc[gi][h * S + t0 - P:h * S + t0, :],
                                        [("vsc", gi, h)], ["avp"], f"k_vp{gi}")
                                else:
                                    wn = npv * d
                                    dma("pool", kst[:, 0:wn], ksc[gi][h * P:(h + 1) * P, t0 - wn:t0],
                                        [("ksc", gi, h)], ["kst"], "k_kst")
                                    cp("pool", akp[:, 0:d, 0:npv], kst[:, 0:wn].rearrange("e (m r) -> e r m", r=d),
                                       ["kst"], ["akp"])
                                    vv = vsc[gi][h * S + t0 - wn:h * S + t0, :].rearrange("(m r) e -> m r e", r=d)
                                    dma("pool", avp[0:npv, 0:d, :], vv,
                                        [("vsc", gi, h)], ["avp"], f"k_vp{gi}")
                            for which in range(2):
                                b = proj(l, xnk, KC, ["xn"])
                                rg = rgq if which == 0 else rgk
                                gofs = (O_QN if which == 0 else O_KN) + gi
                                act(th[:, 4, :], pbank[b][:, :], AF.Copy, [("pb", b)], [("th", 4)])
                                act(tf[:, 0, :], pbank[b][:, :], AF.Square, [("pb", b)], [("tf", 0)])
                                mm(pbank[2][:, :], ones_f, tf[:, 0, :], True, True, [("tf", 0), "cst"], [("pb", 2)])
                                mm(pbank[3][:, :], rg[:, gi, :], th[:, 4, :], True, True, [("th", 4), "rg"], [("pb", 3)])
                                rstd_from(("pb", 2), pbank[2][:, :], P, tf[:, 1, :], ("tf", 1))
                                stt("dve", tf[:, 2, :], pbank[b][:, :], cols[:, l, gofs:gofs + 1], Ct[:, :], ALU.mult, ALU.mult,
                                    [("pb", b), "cols", "Ct"], [("tf", 2)])
                                tt("dve", tf[:, 3, :], pbank[3][:, :], Snt[:, :], ALU.mult, [("pb", 3), "Snt"], [("tf", 3)])
                                tt("dve", tf[:, 2, :], tf[:, 2, :], tf[:, 3, :], ALU.add, [("tf", 2), ("tf", 3)], [("tf", 2)])
                                dsti = 5 + which
                                if d > 1:
                                    o_ap = th[:, dsti, :].rearrange("e (r m) -> e m r", r=d)
                                    a_ap = tf[:, 2, :].rearrange("e (m r) -> e m r", r=d)
                                    b_ap = tf[:, 1, :].rearrange("e (m r) -> e m r", r=d)
                                else:
                                    o_ap, a_ap, b_ap = th[:, dsti, :], tf[:, 2, :], tf[:, 1, :]
                                if which == 1:
                                    tt("dve", th[:, 0, :], tf[:, 2, :], tf[:, 1, :], ALU.mult, [("tf", 2), ("tf", 1)], [("th", 0)])
                                    if d > 1:
                                        cp("pool", o_ap, th[:, 0, :].rearrange("e (m r) -> e m r", r=d), [("th", 0)], [("th", dsti)])
                                    else:
                                        cp("pool", o_ap, th[:, 0, :], [("th", 0)], [("th", dsti)])
                                else:
                                    tt("dve", o_ap, a_ap, b_ap, ALU.mult, [("tf", 2), ("tf", 1)], [("th", dsti)])
                            dma("pool", ksc[gi][h * P:(h + 1) * P, t0:t0 + T], th[:, 0, :], [("th", 0)], [("ksc", gi, h)], f"k_ks{gi}")
                            b = proj(l, xnk, KC, ["xn"])
                            if d > 1:
                                o_ap = th[:, 7, :].rearrange("e (r m) -> e m r", r=d)
                                a_ap = pbank[b][:, :].rearrange("e (m r) -> e m r", r=d)
                            else:
                                o_ap, a_ap = th[:, 7, :], pbank[b][:, :]
                            cp("act", o_ap, a_ap, [("pb", b)], [("th", 7)])
                            for rnd in range(ntl // 4 if ntl > 8 else 1):
                                lo = rnd * (ntl if ntl <= 8 else 4)
                                cnt = ntl if ntl <= 8 else 4
                                for i in range(cnt):
                                    tl = lo + i
                                    tp(ptr[:nq, i * P:(i + 1) * P], th[:, 7, tl * nq:(tl + 1) * nq], [("th", 7), "cstb"], ["ptr"])
                                cp("act", avt[:nq, lo:lo + cnt, :], ptr[:nq, 0:cnt * P].rearrange("p (c k) -> p c k", k=P),
                                   ["ptr"], ["avt"])
                            if d == 1:
                                vv = vsc[gi][h * S + t0:h * S + t0 + T, :].rearrange("(j p) e -> p j e", p=P)
                                dma("pool", vv, avt[:, 0:4, :], ["avt"], [("vsc", gi, h)], f"k_vs{gi}")
                            else:
                                vv = vsc[gi][h * S + t0:h * S + t0 + T, :].rearrange("(m r) e -> m r e", r=d)
                                dma("pool", vv, avt[:nq, 0:d, :], ["avt"], [("vsc", gi, h)], f"k_vs{gi}")
                            sc = float(P) ** -0.5
                            if gi == 0:
                                ck(69)
                            for tl in range(ntl):
                                qs = slice(tl * nq, (tl + 1) * nq)
                                pbuf = tl % 2
                                if d == 1:
                                    if tl > 0:
                                        kp, vp, npk = th[:, 6, (tl - 1) * P:tl * P], avt[:, tl - 1, :], P
                                        pr = [("th", 6), "avt"]
                                    elif npv > 0:
                                        kp, vp, npk = akp[:, 0, :], avp[:, 0, :], P
                                        pr = ["akp", "avp"]
                                    else:
                                        kp, vp, npk, pr = None, None, 0, []
                                else:
                                    if npv > 0:
                                        kp, vp, npk = akp[:, tl, 0:npv], avp[0:npv, tl, :], npv
                                        pr = ["akp", "avp"]
                                    else:
                                        kp, vp, npk, pr = None, None, 0, []
                                sb_ = 3
                                if npk:
                                    mm(pbank[sb_][:npk, 0:nq], kp, th[:, 5, qs], True, True, pr + [("th", 5)], [("pb", 3)])
                                    if gi == 0 and tl == 0:
                                        ck(80)
                                    act(pt_[:npk, pbuf, 0, :nq], pbank[sb_][:npk, 0:nq], AF.Exp, [("pb", 3)], [("pt", pbuf, 0)], scale=sc)
                                    if gi == 0 and tl == 0:
                                        ck(81)
                                    if npk == P:
                                        tt(MASKENG, pt_[:npk, pbuf, 0, :nq], pt_[:npk, pbuf, 0, :nq], uprev_b[:npk, :nq], ALU.mult,
                                           [("pt", pbuf, 0), "cstb"], [("pt", pbuf, 0)])
                                if gi == 0 and tl == 0:
                                    ck(82)
                                mm(pbank[4][:nq, 0:nq], th[:, 6, qs], th[:, 5, qs], True, True, [("th", 6), ("th", 5)], [("pb", 4)])
                                act(pt_[:nq, pbuf, 1, :nq], pbank[4][:nq, 0:nq], AF.Exp, [("pb", 4)], [("pt", pbuf, 1)], scale=sc)
                                tt(MASKENG, pt_[:nq, pbuf, 1, :nq], pt_[:nq, pbuf, 1, :nq], ucur_b[:nq, :nq], ALU.mult,
                                   [("pt", pbuf, 1), "cstb"], [("pt", pbuf, 1)])
                                if gi == 0 and tl == 0:
                                    ck(83)
                                if npk:
                                    mm(pbank[5][:, qs], vp, pt_[:npk, pbuf, 0, :nq], True, False, pr + [("pt", pbuf, 0)], [("pb", 5)])
                                    mm(pbank[6][:, qs], ones_b[:npk, :], pt_[:npk, pbuf, 0, :nq], True, False, [("pt", pbuf, 0), "cstb"], [("pb", 6)])
                                if gi == 0 and tl == 0:
                                    ck(84)
                                mm(pbank[5][:, qs], avt[:nq, tl, :], pt_[:nq, pbuf, 1, :nq], not npk, True, ["avt", ("pt", pbuf, 1)], [("pb", 5)])
                                mm(pbank[6][:, qs], ones_b[:nq, :], pt_[:nq, pbuf, 1, :nq], not npk, True, [("pt", pbuf, 1), "cstb"], [("pb", 6)])
                                if gi == 0:
                                    ck(70 + tl)
                            if d > 1:
                                o_nat = pbank[5][:, :].rearrange("e (r m) -> e m r", r=d)
                                d_nat = pbank[6][:, :].rearrange("e (r m) -> e m r", r=d)
                                n_ap = tf[:, 8, :].rearrange("e (m r) -> e m r", r=d)
                                dn_ap = tf[:, 9, :].rearrange("e (m r) -> e m r", r=d)
                            else:
                                o_nat, d_nat, n_ap, dn_ap = pbank[5][:, :], pbank[6][:, :], tf[:, 8, :], tf[:, 9, :]
                            ck(50 + gi)
                            if gi == 0:
                                cp("dve", n_ap, o_nat, [("pb", 5)], [("tf", 8)])
                                cp("dve", dn_ap, d_nat, [("pb", 6)], [("tf", 9)])
                            else:
                                tt("dve", n_ap, n_ap, o_nat, ALU.add, [("pb", 5), ("tf", 8)], [("tf", 8)])
                                tt("dve", dn_ap, dn_ap, d_nat, ALU.add, [("pb", 6), ("tf", 9)], [("tf", 9)])
                        tr.add("dve", lambda e: e.reciprocal(tf[:, 9, :], tf[:, 9, :]), r=[("tf", 9)], w=[("tf", 9)])
                        tt("dve", ys[:, NB + h, :], tf[:, 8, :], tf[:, 9, :], ALU.mult, [("tf", 8), ("tf", 9)], ["ys"])

                        ck(6)
                        b = proj(l, xnk, KC, ["xn"])
                        act(uz[:, h, :], pbank[b][:, :], AF.Gelu, [("pb", b)], ["uz"])
                        b = proj(l, xnk, KC, ["xn"])
                        act(vz[:, h, :], pbank[b][:, :], AF.Gelu, [("pb", b)], ["vz"])

                    for h in range(NB):
                        mm(pbank[5][:, :], ones_b, vz[:, h, :], h == 0, h == NB - 1, ["vz", "cstb"], [("pb", 5)])
                    for h in range(NB):
                        s = h % 2
                        act(tf[:, s, :], vz[:, h, :], AF.Square, ["vz"], [("tf", s)])
                        mm(pbank[6][:, :], ones_f, tf[:, s, :], h == 0, h == NB - 1, [("tf", s), "cst"], [("pb", 6)])
                    tsc("dve", tf[:, 2, :], pbank[5][:, :], 1.0 / BW, None, ALU.mult, None, [("pb", 5)], [("tf", 2)])
                    tt("dve", tf[:, 3, :], tf[:, 2, :], tf[:, 2, :], ALU.mult, [("tf", 2)], [("tf", 3)])
                    stt("dve", tf[:, 3, :], pbank[6][:, :], 1.0 / BW, tf[:, 3, :], ALU.mult, ALU.subtract, [("pb", 6), ("tf", 3)], [("tf", 3)])
                    act(tf[:, 3, :], tf[:, 3, :], AF.Sqrt, [("tf", 3), "epsc"], [("tf", 3)], bias=epsc[:, 0:1])
                    tr.add("dve", lambda e: e.reciprocal(tf[:, 3, :], tf[:, 3, :]), r=[("tf", 3)], w=[("tf", 3)])
                    for h in range(NB):
                        tt("dve", tf[:, 4, :], vz[:, h, :], tf[:, 2, :], ALU.subtract, ["vz", ("tf", 2)], [("tf", 4)])
                        tt("dve", tf[:, 4, :], tf[:, 4, :], tf[:, 3, :], ALU.mult, [("tf", 4), ("tf", 3)], [("tf", 4)])
                        tsc("dve", th[:, 0, :], tf[:, 4, :], cols[:, l, O_LG + h:O_LG + h + 1], cols[:, l, O_LB + h:O_LB + h + 1],
                            ALU.mult, ALU.add, [("tf", 4), "cols"], [("th", 0)])
                        for c in range(4):
                            tp(ptr[:, c * P:(c + 1) * P], th[:, 0, c * P:(c + 1) * P], [("th", 0), "cstb"], ["ptr"])
                        cp("act", th[:, 1, :], ptr[:, 0:T], ["ptr"], [("th", 1)])
                        for c in range(4):
                            mm(pbank[4][:, c * P:(c + 1) * P], th[:, 1, c * P:(c + 1) * P], sgw[:, h * P:(h + 1) * P], True, True,
                               [("th", 1), "sgw"], [("pb", 4)])
                        tt("dve", tf[:, 5, :].rearrange("p (c t) -> p c t", t=P), pbank[4][:, :].rearrange("p (c t) -> p c t", t=P),
                           sgb[:, h * P:(h + 1) * P].unsqueeze(1).to_broadcast([P, 4, P]), ALU.add, [("pb", 4), "sgb"], [("tf", 5)])
                        tt("dve", ys[:, 2 * NB + h, :], tf[:, 5, :], uz[:, h, :], ALU.mult, [("tf", 5), "uz"], ["ys"])

                    if DEBUG[0] and l == 0:
                        dma("pool", dbg.rearrange("(c p) t -> p c t", p=P)[:, :, t0:t0 + T], ys[:, :, :], ["ys"], [], "k_dbg")
                    ck(7)
                    for j in range(KC):
                        bslot = wload(l)
                        for i in range(3):
                            bb = 4 + i
                            for kc in range(NB):
                                o = (i * NB + kc) * P
                                mm(pbank[bb][:, :], wsl[:, bslot, o:o + P], ys[:, i * NB + kc, :], kc == 0, kc == NB - 1,
                                   [("wsl", bslot), "ys"], [("pb", bb)])
                        for i in range(3):
                            b = proj(l, xnk, KC, ["xn"])
                            act(tf[:, i, :], pbank[b][:, :], AF.Sigmoid, [("pb", b)], [("tf", i)])
                        for i in range(3):
                            bb = 4 + i
                            if i == 0:
                                tt("dve", tf[:, 3, :], tf[:, 0, :], pbank[bb][:, :], ALU.mult, [("tf", 0), ("pb", bb)], [("tf", 3)])
                            else:
                                tt("dve", tf[:, 4, :], tf[:, i, :], pbank[bb][:, :], ALU.mult, [("tf", i), ("pb", bb)], [("tf", 4)])
                                if i == 1:
                                    tt("dve", tf[:, 3, :], tf[:, 3, :], tf[:, 4, :], ALU.add, [("tf", 3), ("tf", 4)], [("tf", 3)])
                                else:
                                    tt("dve", mh[:, j, :], tf[:, 3, :], tf[:, 4, :], ALU.add, [("tf", 3), ("tf", 4)], ["mh"])
                    ck(8)
                    tr.barrier()
                    for k0 in range(0, KC, 4):
                        dma("pool", xres[:, k0:k0 + 4, :], src.rearrange("(kc p) t -> p kc t", p=P)[:, k0:k0 + 4, t0:t0 + T],
                            [("xd", l)], ["xres"], "k_x")
                    if l + 1 < NL and pending_cast[l + 1]:
                        per = (len(cast_chunks(l + 1)) + NTB - 2) // max(1, NTB - 1)
                        for _ in range(per if tb < NTB - 1 else len(pending_cast[l + 1])):
                            if pending_cast[l + 1]:
                                issue_cast(pending_cast[l + 1].pop(0))
                    for j in range(KC):
                        b = proj(l, lambda kc: mh[:, kc, :], KC, ["mh"])
                        tt("dve", xres[:, j, :], xres[:, j, :], pbank[b][:, :], ALU.add, [("pb", b), "xres"], ["xres"])
                    ck(9)
                    norm_block(l, O_GF)
                    for pt in range(3):
                        for jb in range(KC):
                            fb = pt * KC + jb
                            bg = proj(l, xnk, KC, ["xn"])
                            cp(MASKENG, gext[:, 0:2], carry[:, fb, :], ["carry"], ["gext"])
                            cp("act", gext[:, 2:T + 2], pbank[bg][:, :], [("pb", bg)], ["gext"])
                            bv = proj(l, xnk, KC, ["xn"])
                            cwo = O_CW + fb
                            tsc("dve", tf[:, 0, :], gext[:, 2:T + 2], cols[:, l, cwo + 2 * FC:cwo + 2 * FC + 1],
                                cols[:, l, O_CB + fb:O_CB + fb + 1], ALU.mult, ALU.add, ["gext", "cols"], [("tf", 0)])
                            stt("dve", tf[:, 0, :], gext[:, 1:T + 1], cols[:, l, cwo + FC:cwo + FC + 1], tf[:, 0, :],
                                ALU.mult, ALU.add, ["gext", "cols", ("tf", 0)], [("tf", 0)])
                            stt("dve", tf[:, 0, :], gext[:, 0:T], cols[:, l, cwo:cwo + 1], tf[:, 0, :],
                                ALU.mult, ALU.add, ["gext", "cols", ("tf", 0)], [("tf", 0)])
                            cp(MASKENG, carry[:, fb, :], gext[:, T:T + 2], ["gext"], ["carry"])
                            act(tf[:, 1, :], tf[:, 0, :], AF.Silu, [("tf", 0)], [("tf", 1)])
                            tt("dve", mh[:, jb, :], tf[:, 1, :], pbank[bv][:, :], ALU.mult, [("tf", 1), ("pb", bv)], ["mh"])
                        for j in range(KC):
                            b = proj(l, lambda kc: mh[:, kc, :], KC, ["mh"])
                            tt("dve", xres[:, j, :], xres[:, j, :], pbank[b][:, :], ALU.add, [("pb", b), "xres"], ["xres"])
                    ck(10)
                    for k0 in range(0, KC, 4):
                        dma("pool", dst.rearrange("(kc p) t -> p kc t", p=P)[:, k0:k0 + 4, t0:t0 + T], xres[:, k0:k0 + 4, :],
                            ["xres"], [("xd", l + 1)], "k_xo")
                    ck(11)
                    if tb == 1:
                        ck(12)
                    if tb == NTB - 1:
                        ck(13)
        except _Stop:
            pass

        tr.finalize()
        tr.simulate()
        if DEBUG[0] >= 2:
            tr.dump_tail()
        keys = set(Tracker.ENG) | set(tr.dma_cnt.keys())
        sems = {k: es.enter_context(nc.semaphore(f"s_{k}")) for k in sorted(keys)}
        with nc.Block() as block:
            @block.tensor
            def _(e):
                tr.emit("pe", e, sems)

            @block.scalar
            def _(e):
                tr.emit("act", e, sems)

            @block.vector
            def _(e):
                tr.emit("dve", e, sems)

            @block.gpsimd
            def _(e):
                tr.emit("pool", e, sems)

            @block.sync
            def _(e):
                tr.emit("sp", e, sems)
    return nc


def make_inputs(D, S, NL, b, x, positions, norm_mix, w_in, hgrn_lower_bounds, hgrn_out_norm, q_norm, k_norm,
                sg_ln_g, sg_ln_b, sg_w, sg_b, w_gate, w_branch, w_out, norm_ffn, w_up, ffn_conv_w, ffn_conv_b, w_down,
                shared=None):
    KC, BW, NB, DFF, FC, NTB, NT = _dims(D, S)
    if shared is None:
        shared = {}
        shared["consts"] = make_consts_full()
        shared["cols"] = np.concatenate([pack_cols(D, S, l, norm_mix, hgrn_out_norm, q_norm, k_norm, sg_ln_g, sg_ln_b,
                                                   norm_ffn, ffn_conv_w, ffn_conv_b) for l in range(NL)], axis=0)
        LBW = max(NB, 8)
        lbraw = np.zeros((P, NL, LBW), np.float32)
        for l in range(NL):
            lbraw[:, l, :NB] = hgrn_lower_bounds[l].reshape(NB, P).T
        shared["lbraw"] = lbraw.reshape(P, NL * LBW)
        shared["sgb"] = np.ascontiguousarray(np.broadcast_to(sg_b.reshape(NL, 1, NB * P), (NL, P, NB * P)).reshape(NL * P, NB * P).astype(np.float32))
        shared["sgwT"] = np.ascontiguousarray(sg_w.transpose(0, 3, 1, 2).reshape(NL * P, NB * P).astype(np.float32))
        shared["wall"] = np.concatenate([pack_weights(D, S, l, w_in, w_gate, w_branch, w_out, w_up, w_down)
                                         for l in range(NL)], axis=0).reshape(NL * NT * P, KC * P)
    m = dict(shared)
    m["xT"] = np.ascontiguousarray(x[b].T)
    m["pos"] = np.ascontiguousarray(np.broadcast_to(positions[b].reshape(1, S), (P, S)).astype(np.int32))
    return m, shared


def run(D, S, NL, inputs):
    inputs = {k: np.asarray(v) for k, v in inputs.items()}
    B = inputs["x"].shape[0]
    nc = build(D, S, NL)
    in_maps = []
    shared = None
    for b in range(B):
        m, shared = make_inputs(D, S, NL, b, shared=shared, **inputs)
        in_maps.append(m)
    res = run_bass_kernel_spmd(nc, in_maps, core_ids=list(range(B)))
    LAST["res"] = res.results
    out = np.stack([np.ascontiguousarray(res.results[b]["outT"].T) for b in range(B)], axis=0)
    return out.astype(np.float32)


def kernel(**inputs):
    return run(4096, 4096, 2, inputs)
```

```python
import math
import numpy as np
import concourse.bass as bass
import concourse.mybir as mybir
from concourse.bass_utils import run_bass_kernel_spmd

F32 = mybir.dt.float32
BF16 = mybir.dt.bfloat16
I32 = mybir.dt.int32
AF = mybir.ActivationFunctionType
ALU = mybir.AluOpType
P = 128
T = 512
EPS = 1e-6
ROPE_THETA = 500000.0
DILS = (1, 4, 16)
SAME_ENGINE_SYNC = True
MASKENG = "pool"


class _Op:
    __slots__ = ("eng", "fn", "deps", "key", "val", "is_dma", "needs_inc", "idx", "desc")


class Tracker:
    ENG = ("pe", "act", "dve", "pool", "sp")

    def __init__(self):
        self.ops = {e: [] for e in self.ENG}
        self.lastw = {}
        self.readers = {}
        self.dma_cnt = {}
        self.last_dma = {}

    def add(self, eng, fn, r=(), w=(), dma_key=None):
        op = _Op()
        op.eng = eng
        op.fn = fn
        op.is_dma = dma_key is not None
        op.key = dma_key if op.is_dma else eng
        op.needs_inc = op.is_dma
        op.val = 0
        deps = {}

        def dep(o):
            if o is None or o is op:
                return
            if o.is_dma:
                o = self.last_dma[o.key]
                if o is op:
                    return
            k = o.key
            if k not in deps or deps[k].idx < o.idx:
                deps[k] = o

        for k in r:
            dep(self.lastw.get(k))
        for k in w:
            dep(self.lastw.get(k))
            for o in self.readers.get(k, {}).values():
                dep(o)
        if op.is_dma:
            n = self.dma_cnt.get(dma_key, 0) + 1
            self.dma_cnt[dma_key] = n
            op.idx = n
            op.val = 16 * n
            self.last_dma[dma_key] = op
        else:
            op.idx = len(self.ops[eng])
        op.deps = list(deps.values())
        op.desc = (list(r), list(w))
        for k in r:
            self.readers.setdefault(k, {})[op.key] = op
        for k in w:
            self.lastw[k] = op
            self.readers[k] = {}
        self.ops[eng].append(op)
        return op

    def barrier(self):
        lasts = []
        for e in self.ENG:
            for o in reversed(self.ops[e]):
                if not o.is_dma:
                    lasts.append(o)
                    break
        lasts += list(self.last_dma.values())
        for e in self.ENG:
            op = _Op()
            op.eng = e
            op.fn = None
            op.is_dma = False
            op.key = e
            op.needs_inc = False
            op.val = 0
            op.idx = len(self.ops[e])
            op.deps = [o for o in lasts if not (o.key == e)]
            op.desc = ('barrier',)
            self.ops[e].append(op)
        self.lastw = {}
        self.readers = {}

    def finalize(self):
        for e in self.ENG:
            for op in self.ops[e]:
                for d in op.deps:
                    if not d.is_dma and (d.eng != op.eng or op.is_dma or (SAME_ENGINE_SYNC and d.eng != "pe")):
                        d.needs_inc = True
        for e in self.ENG:
            c = 0
            for op in self.ops[e]:
                if not op.is_dma and op.needs_inc:
                    if op.fn is None:
                        op.needs_inc = False
                        continue
                    c += 1
                    op.val = c


    def simulate(self):
        sem = {}
        pc = {e: 0 for e in self.ENG}
        progress = True
        while progress:
            progress = False
            for e in self.ENG:
                while pc[e] < len(self.ops[e]):
                    op = self.ops[e][pc[e]]
                    ok = True
                    for d in op.deps:
                        if not d.is_dma and d.eng == op.eng and not op.is_dma and not (SAME_ENGINE_SYNC and d.eng != "pe"):
                            continue
                        if d.val <= 0:
                            continue
                        if sem.get(d.key, 0) < d.val:
                            ok = False
                            break
                    if not ok:
                        break
                    if op.fn is not None:
                        if op.is_dma:
                            sem[op.key] = sem.get(op.key, 0) + 16
                        elif op.needs_inc:
                            sem[op.key] = sem.get(op.key, 0) + 1
                    pc[e] += 1
                    progress = True
        stuck = {e: (pc[e], len(self.ops[e])) for e in self.ENG if pc[e] < len(self.ops[e])}
        if stuck:
            msg = []
            for e, (i, n) in stuck.items():
                op = self.ops[e][i]
                msg.append((e, i, n, [(d.key, d.val, sem.get(d.key, 0)) for d in op.deps if d.val > 0 and sem.get(d.key, 0) < d.val]))
            raise RuntimeError(f"tracker deadlock: {msg}")


    def dump_tail(self, n=14):
        for e in self.ENG:
            print("==== engine", e, len(self.ops[e]))
            waited = {}
            rows = []
            for op in self.ops[e]:
                ws = []
                for d in op.deps:
                    if not d.is_dma and d.eng == op.eng and not op.is_dma and not (SAME_ENGINE_SYNC and d.eng != "pe"):
                        continue
                    if d.val <= 0 or waited.get(d.key, 0) >= d.val:
                        continue
                    waited[d.key] = d.val
                    ws.append((d.key, d.val))
                inc = (op.key, 16 if op.is_dma else 1, op.val) if (op.fn is not None and (op.is_dma or op.needs_inc)) else None
                rows.append((ws, inc, op.desc))
            for r_ in rows[-n:]:
                print("   wait", r_[0], "| inc", r_[1], "|", r_[2])

    def emit(self, eng, e, sems):
        waited = {}
        for op in self.ops[eng]:
            for d in op.deps:
                if not d.is_dma and d.eng == op.eng and not op.is_dma and not (SAME_ENGINE_SYNC and d.eng != "pe"):
                    continue
                if d.val <= 0:
                    continue
                if waited.get(d.key, 0) >= d.val:
                    continue
                waited[d.key] = d.val
                e.wait_ge(sems[d.key], d.val)
            if op.fn is None:
                continue
            ins = op.fn(e)
            if op.is_dma:
                ins.then_inc(sems[op.key], 16)
            elif op.needs_inc:
                ins.then_inc(sems[op.key], 1)
        if eng == "sp":
            for k, o in self.last_dma.items():
                if waited.get(k, 0) < o.val:
                    e.wait_ge(sems[k], o.val)


def _tile_cols(W):
    K, N = W.shape
    return np.ascontiguousarray(W.reshape(K // P, P, N // P, P).transpose(2, 1, 0, 3))


def _dims(D, S):
    KC = D // P
    BW = D // 4
    NB = BW // P
    DFF = 3 * D
    FC = DFF // P
    NTB = S // T
    NT = NB * 15 + KC + KC * 3 + KC + 3 * (3 * KC)
    return KC, BW, NB, DFF, FC, NTB, NT


def _in_cols(BW, h):
    A = 4 * BW
    Bc = 9 * BW
    cols = [1 * BW, 0 * BW, 2 * BW, 3 * BW]
    for gi in range(3):
        for j in range(3):
            cols.append(A + gi * 3 * BW + j * BW)
    cols += [A + Bc, A + Bc + BW]
    return [c + h * P for c in cols]


def pack_weights(D, S, l, w_in, w_gate, w_branch, w_out, w_up, w_down):
    KC, BW, NB, DFF, FC, NTB, NT = _dims(D, S)
    WT = KC * P
    out = np.empty((NT, P, WT), np.float32)
    t = 0
    win_t = _tile_cols(w_in[l])
    for h in range(NB):
        for c in _in_cols(BW, h):
            out[t] = win_t[c // P].reshape(P, WT)
            t += 1
    wg_t = _tile_cols(w_gate[l])
    wb_t = [_tile_cols(w_branch[l, i]) for i in range(3)]
    for j in range(KC):
        out[t] = 0.0
        out[t].reshape(P, -1)[:, :3 * NB * P] = np.stack([wb_t[i][j] for i in range(3)], axis=1).reshape(P, 3 * NB * P)
        t += 1
        for i in range(3):
            out[t] = wg_t[i * KC + j].reshape(P, WT)
            t += 1
    wo_t = _tile_cols(w_out[l])
    for j in range(KC):
        out[t] = wo_t[j].reshape(P, WT)
        t += 1
    wu_t = _tile_cols(w_up[l])
    for pt in range(3):
        for jb in range(KC):
            out[t] = wu_t[pt * KC + jb].reshape(P, WT)
            t += 1
            out[t] = wu_t[FC + pt * KC + jb].reshape(P, WT)
            t += 1
        wd_t = _tile_cols(w_down[l][pt * D:(pt + 1) * D])
        for j in range(KC):
            out[t] = wd_t[j].reshape(P, WT)
            t += 1
    assert t == NT
    return out


def make_consts_full():
    c = np.zeros((P, 10 * P), np.float32)
    i = np.arange(P)
    c[:, 0:P] = np.eye(P)
    c[:, P:2 * P] = (i[:, None] <= i[None, :])
    c[:, 2 * P:3 * P] = (i[:, None] >= i[None, :])
    R = np.zeros((P, P), np.float32)
    for e in range(16):
        R[e + 16, e] = -1.0
        R[e, e + 16] = 1.0
    c[:, 3 * P:4 * P] = R
    c[:, 4 * P:5 * P] = 1.0
    m = np.ones(T, np.float32)
    m[::64] = 0.0
    c[:, 5 * P:5 * P + T] = m[None, :]
    inv = np.zeros(P, np.float32)
    j = np.arange(16, dtype=np.float32)
    invf = np.power(np.float32(ROPE_THETA), -(2 * j) / np.float32(32)).astype(np.float32)
    inv[0:16] = invf
    inv[16:32] = invf
    c[:, 9 * P] = inv
    return c


def pack_cols(D, S, l, norm_mix, hgrn_out_norm, q_norm, k_norm, sg_ln_g, sg_ln_b, norm_ffn, ffn_conv_w, ffn_conv_b):
    KC, BW, NB, DFF, FC, NTB, NT = _dims(D, S)
    parts = [norm_mix[l].reshape(KC, P).T, norm_ffn[l].reshape(KC, P).T, hgrn_out_norm[l].reshape(1, P).T,
             q_norm[l].T, k_norm[l].T, sg_ln_g[l].reshape(NB, P).T, sg_ln_b[l].reshape(NB, P).T,
             ffn_conv_w[l, 0].reshape(FC, P).T, ffn_conv_w[l, 1].reshape(FC, P).T, ffn_conv_w[l, 2].reshape(FC, P).T,
             ffn_conv_b[l].reshape(FC, P).T]
    return np.ascontiguousarray(np.concatenate(parts, axis=1).astype(np.float32))


STOP = [0]
DEBUG = [0]
LAST = {}


class _Stop(Exception):
    pass


CUR = [0]


def ck(n):
    if STOP[0] == n + 100 * CUR[0]:
        raise _Stop()


def build(D, S, NL):
    KC, BW, NB, DFF, FC, NTB, NT = _dims(D, S)
    WT = KC * P
    NCP = 2 * KC + 1 + 6 + 2 * NB + 4 * FC
    nc = bass.Bass("TRN2", target_bir_lowering=False)
    try:
        nc.allow_low_precision("bf16 matmul operands with fp32 accumulation (reference bar is bf16-level)")
    except Exception:
        pass
    xT = nc.dram_tensor("xT", [D, S], F32, kind="ExternalInput").ap()
    posd = nc.dram_tensor("pos", [P, S], I32, kind="ExternalInput").ap()
    constd = nc.dram_tensor("consts", [P, 10 * P], F32, kind="ExternalInput").ap()
    colsd = nc.dram_tensor("cols", [NL * P, NCP], F32, kind="ExternalInput").ap()
    LBW = max(NB, 8)
    lbd = nc.dram_tensor("lbraw", [P, NL * LBW], F32, kind="ExternalInput").ap()
    sgbd = nc.dram_tensor("sgb", [NL * P, NB * P], F32, kind="ExternalInput").ap()
    sgwd = nc.dram_tensor("sgwT", [NL * P, NB * P], F32, kind="ExternalInput").ap()
    wall = nc.dram_tensor("wall", [NL * NT * P, WT], F32, kind="ExternalInput").ap()
    outT = nc.dram_tensor("outT", [D, S], F32, kind="ExternalOutput").ap()
    dbg = nc.dram_tensor("dbg", [3 * NB * P, S], BF16, kind="ExternalOutput").ap() if DEBUG[0] else None
    dbg2 = nc.dram_tensor("dbg2", [P, 1024], F32, kind="ExternalOutput").ap() if DEBUG[0] else None
    TPC = 128
    NCH = (NL * NT + TPC - 1) // TPC
    w16c = [nc.dram_tensor(f"w16_{c}", [min(TPC, NL * NT - c * TPC) * P, WT], BF16).ap() for c in range(NCH)]
    xmid = nc.dram_tensor("xmid", [D, S], F32).ap()
    ksc = [nc.dram_tensor(f"ksc{g}", [NB * P, S], BF16).ap() for g in range(3)]
    vsc = [nc.dram_tensor(f"vsc{g}", [NB * S, P], BF16).ap() for g in range(3)]

    tr = Tracker()
    import contextlib
    es = contextlib.ExitStack()

    def sb(name, shape, dt):
        return es.enter_context(nc.sbuf_tensor(name, shape, dt))

    def ps(name, shape, dt):
        return es.enter_context(nc.psum_tensor(name, shape, dt))

    with es:
        NSLOT = 2

        class Slots:
            def __init__(self, aps):
                self.aps = aps

            def __getitem__(self, k):
                p_, i_, f_ = k
                return self.aps[i_][p_, f_]

        TW = 5 * T + 8 * T // 2 + 2 * NB * T // 2 + 3 * 16 * P // 2 + 2 * 8 * P // 2 + 8 * P // 2 + 2 * 64 // 2 + 2 * T + NB * P + 16 * P // 2
        AW = max(KC * T, TW)
        arena = sb("arena", [P, AW], F32)
        xres = arena[:, 0:KC * T].rearrange("p (k t) -> p k t", t=T)
        _o = [0]

        def carve(words):
            a0 = _o[0]
            _o[0] += words
            assert _o[0] <= AW
            return arena[:, a0:a0 + words]

        tfa = sb("tfa", [P, 5, T], F32)
        tf = Slots([tfa[:, i, :] for i in range(5)] + [carve(T) for _ in range(5)])
        th = Slots([carve(T // 2).bitcast(BF16) for _ in range(8)])
        uz = carve(NB * T // 2).bitcast(BF16).rearrange("p (n t) -> p n t", t=T)
        vz = carve(NB * T // 2).bitcast(BF16).rearrange("p (n t) -> p n t", t=T)
        avt = carve(16 * P // 2).bitcast(BF16).rearrange("p (n t) -> p n t", t=P)
        avp = carve(16 * P // 2).bitcast(BF16).rearrange("p (n t) -> p n t", t=P)
        akp = carve(16 * P // 2).bitcast(BF16).rearrange("p (n t) -> p n t", t=P)
        ktm = carve(8 * P // 2).bitcast(BF16).rearrange("p (n t) -> p n t", t=P)
        vtm = carve(8 * P // 2).bitcast(BF16).rearrange("p (n t) -> p n t", t=P)
        pt_ = carve(4 * P // 2).bitcast(BF16).rearrange("p (a b t) -> p a b t", a=2, b=2)
        atm = carve(64).bitcast(BF16).rearrange("p (a t) -> p a t", a=2)
        Ct = carve(T)
        Snt = carve(T)
        sgwf = carve(NB * P)
        kst = carve(16 * P // 2).bitcast(BF16)
        xn = sb("xn", [P, KC, T], BF16)
        ys = sb("ys", [P, 3 * NB, T], BF16)
        mh = sb("mh", [P, KC, T], BF16)
        wsl = sb("wsl", [P, NSLOT, WT], BF16)
        cst = sb("cst", [P, 10 * P], F32)
        cstb = sb("cstb", [P, 5 * P], BF16)
        cols = sb("cols_s", [P, NL, NCP], F32)
        lbr = sb("lbr_s", [P, NL * LBW], F32)
        lbc = sb("lbc", [P, 2 * LBW], F32)
        sgb = sb("sgb_s", [P, NB * P], F32)
        sgw = sb("sgw", [P, NB * P], BF16)
        rgq = sb("rgq", [P, 3, P], BF16)
        rgk = sb("rgk", [P, 3, P], BF16)
        Sst = sb("Sst", [P, NB, P], F32)
        Sbf = sb("Sbf", [P, NB, P], BF16)
        carry = sb("carry", [P, FC, 2], F32)
        gext = sb("gext", [P, T + 2], F32)
        dch = sb("dch", [P, 8], F32)
        bl = sb("bl", [P, 8], F32)
        posi = tfa[:, 4, :].bitcast(I32)
        epsc = sb("epsc", [P, 1], F32)
        sflag = sb("sflag", [P, T], F32)
        mbias = sb("mbias", [P, 2, P], BF16)
        LAST['sbuf_free'] = nc.sbuf_bytes_remaining
        pbank = [ps(f"pb{i}", [P, T], F32) for i in range(7)]
        ptr = ps("ptr", [P, 2 * T], BF16)

        ident_b = cstb[:, 0:P]
        ucur_b = cstb[:, P:2 * P]
        uprev_b = cstb[:, 2 * P:3 * P]
        ones_b = cstb[:, 4 * P:5 * P]
        ones_f = cst[:, 4 * P:5 * P]
        ucur_f = cst[:, P:2 * P]
        smask = cst[:, 5 * P:5 * P + T]
        invcol = cst[:, 9 * P:9 * P + 1]

        def dma(q, out, in_, r, w, key):
            tr.add(q, lambda e: e.dma_start(out=out, in_=in_), r=r, w=w, dma_key=key)

        def mm(out, lhsT, rhs, start, stop, r, w):
            tr.add("pe", lambda e: e.matmul(out, lhsT, rhs, start=start, stop=stop), r=r, w=w)

        def tp(out, in_, r, w):
            tr.add("pe", lambda e: e.transpose(out, in_, ident_b[:in_.shape[0], :in_.shape[0]]), r=r, w=w)

        def act(out, in_, func, r, w, scale=1.0, bias=0.0):
            tr.add("act", lambda e: e.activation(out, in_, func, bias=bias, scale=scale), r=r, w=w)

        def tt(eng, out, a, b, op, r, w):
            tr.add(eng, lambda e: e.tensor_tensor(out, a, b, op), r=r, w=w)

        def tsc(eng, out, a, s1, s2, op0, op1, r, w):
            if s2 is None:
                tr.add(eng, lambda e: e.tensor_scalar(out, a, s1, None, op0), r=r, w=w)
            else:
                tr.add(eng, lambda e: e.tensor_scalar(out, a, s1, s2, op0, op1), r=r, w=w)

        def stt(eng, out, a, s, b, op0, op1, r, w):
            tr.add(eng, lambda e: e.scalar_tensor_tensor(out, a, s, b, op0, op1), r=r, w=w)

        def cp(eng, out, in_, r, w):
            if eng == "act":
                tr.add(eng, lambda e: e.copy(out, in_), r=r, w=w)
            else:
                tr.add(eng, lambda e: e.tensor_copy(out, in_), r=r, w=w)

        dma("sp", cst[:, :], constd[:, :], [], ["cst"], "k_cst")
        dma("sp", cols[:, :, :], colsd.rearrange("(l p) n -> p l n", p=P), [], ["cols"], "k_cst")
        dma("sp", lbr[:, :], lbd[:, :], [], ["lbr"], "k_cst")
        cp("dve", cstb[:, :], cst[:, 0:5 * P], ["cst"], ["cstb"])
        tr.add("dve", lambda e: e.memset(epsc[:, :], EPS), r=[], w=["epsc"])
        tsc("dve", mbias[:, 0, :], cst[:, P:2 * P], -1.0, 30000.0, ALU.add, ALU.mult, ["cst"], ["mbias"])
        tsc("dve", mbias[:, 1, :], cst[:, 2 * P:3 * P], -1.0, 30000.0, ALU.add, ALU.mult, ["cst"], ["mbias"])
        tsc("dve", sflag[:, :], cst[:, 5 * P:5 * P + T], -1.0, 1.0, ALU.mult, ALU.add, ["cst"], ["sflag"])
        CH = 8
        NG = 4
        GT = ((NT + NG - 1) // NG + CH - 1) // CH * CH

        def cast_chunks(l):
            out, ti0 = [], 0
            while ti0 < NT:
                gt = l * NT + ti0
                c_, o_ = gt // TPC, gt % TPC
                n = min(CH, NT - ti0, TPC - o_, GT - (ti0 % GT))
                out.append((l, ti0, n, c_, o_))
                ti0 += n
            return out

        def issue_cast(ch):
            l_, ti0, n, c_, o_ = ch
            gt = l_ * NT + ti0
            dma("pool", w16c[c_][o_ * P:(o_ + n) * P, :], wall[gt * P:(gt + n) * P, :], [], [("w16", l_, ti0 // GT)],
                f"k_c{l_}_{ti0 // GT}")

        for ch in cast_chunks(0):
            issue_cast(ch)
        pending_cast = {l_: cast_chunks(l_) for l_ in range(1, NL)}

        wctr = [0]

        def wload(l):
            i = wctr[0]
            wctr[0] += 1
            slot = i % NSLOT
            gi_ = l * NT + (i % NT)
            row = (gi_ % TPC) * P
            dma("sp", wsl[:, slot, :], w16c[gi_ // TPC][row:row + P, :], [("w16", l, (i % NT) // GT)], [("wsl", slot)], f"k_w{slot}")
            return slot

        pbi = [0]

        def next_pb():
            b = pbi[0] % 2
            pbi[0] += 1
            return b

        def proj(l, rhs_of_kc, nk, rkeys, sub=None):
            slot = wload(l)
            b = next_pb()
            for kc in range(nk):
                mm(pbank[b][:, :], wsl[:, slot, kc * P:(kc + 1) * P], rhs_of_kc(kc), kc == 0, kc == nk - 1,
                   [("wsl", slot)] + rkeys, [("pb", b)])
            return b

        def rstd_from(bank_key, bank_ap, n, out_ap, okey):
            act(out_ap, bank_ap, AF.Sqrt, [bank_key, "epsc"], [okey], scale=1.0 / n, bias=epsc[:, 0:1])
            tr.add("dve", lambda e: e.reciprocal(out_ap, out_ap), r=[okey], w=[okey])

        def norm_block(l, gofs):
            for kc in range(KC):
                s = kc % 2
                act(tf[:, s, :], xres[:, kc, :], AF.Square, ["xres"], [("tf", s)])
                mm(pbank[2][:, :], ones_f, tf[:, s, :], kc == 0, kc == KC - 1, [("tf", s), "cst"], [("pb", 2)])
            rstd_from(("pb", 2), pbank[2][:, :], D, tf[:, 2, :], ("tf", 2))
            for kc in range(KC):
                stt("dve", xn[:, kc, :], xres[:, kc, :], cols[:, l, gofs + kc:gofs + kc + 1], tf[:, 2, :],
                    ALU.mult, ALU.mult, ["xres", ("tf", 2), "cols"], ["xn"])

        O_GM, O_GF, O_HG = 0, KC, 2 * KC
        O_QN, O_KN = 2 * KC + 1, 2 * KC + 4
        O_LG, O_LB = 2 * KC + 7, 2 * KC + 7 + NB
        O_CW, O_CB = 2 * KC + 7 + 2 * NB, 2 * KC + 7 + 2 * NB + 3 * FC

        try:
            for l in range(NL):
                src = xT if l == 0 else xmid
                dst = xmid if l < NL - 1 else outT
                tr.barrier()
                if l == 0:
                    tr.add("dve", lambda e: e.memset(lbc[:, 0:LBW], 0.0), r=[], w=["lbc"])
                    tr.add("dve", lambda e: e.memset(lbc[:, LBW:2 * LBW], 1.0), r=[], w=["lbc"])
                else:
                    assert l == 1 and NL == 2
                    tt("dve", lbc[:, 0:LBW], lbr[:, LBW:2 * LBW], lbr[:, 0:LBW], ALU.subtract, ["lbr"], ["lbc"])
                    act(lbc[:, 0:LBW], lbc[:, 0:LBW], AF.Sigmoid, ["lbc"], ["lbc"])
                    tsc("dve", lbc[:, LBW:2 * LBW], lbc[:, 0:LBW], -1.0, 1.0, ALU.mult, ALU.add, ["lbc"], ["lbc"])
                ck(90)
                tr.add("dve", lambda e: e.memset(Sst[:, :, :], 0.0), r=[], w=["Sst"])
                tr.add("dve", lambda e: e.memset(Sbf[:, :, :], 0.0), r=[], w=["Sbf"])
                tr.add("dve", lambda e: e.memset(carry[:, :, :], 0.0), r=[], w=["carry"])
                ck(91)
                dma("sp", sgb[:, :], sgbd[l * P:(l + 1) * P, :], [], ["sgb"], "k_cst")
                dma("sp", sgwf[:, :], sgwd[l * P:(l + 1) * P, :], [], ["sgwf"], "k_cst")
                ck(92)
                for g in range(NB):
                    tt("dve", sgw[:, g * P:(g + 1) * P], sgwf[:, g * P:(g + 1) * P], ucur_f, ALU.mult,
                       ["sgwf", "cst"], ["sgw"])
                ck(93)
                for gi in range(3):
                    tsc("dve", rgq[:, gi, :], cst[:, 3 * P:4 * P], cols[:, l, O_QN + gi:O_QN + gi + 1], None, ALU.mult, None,
                        ["cst", "cols"], ["rg"])
                    tsc("dve", rgk[:, gi, :], cst[:, 3 * P:4 * P], cols[:, l, O_KN + gi:O_KN + gi + 1], None, ALU.mult, None,
                        ["cst", "cols"], ["rg"])
                tr.barrier()

                ck(2)
                for tb in range(NTB):
                    t0 = tb * T
                    CUR[0] = tb + NTB * l
                    for k0 in range(0, KC, 4):
                        dma("pool", xres[:, k0:k0 + 4, :], src.rearrange("(kc p) t -> p kc t", p=P)[:, k0:k0 + 4, t0:t0 + T],
                            [("xd", l)], ["xres"], "k_x")
                    norm_block(l, O_GM)
                    tr.barrier()
                    ck(3)
                    dma("pool", posi[:, :], posd[:, t0:t0 + T], [], [("tf", 4)], "k_pos")
                    cp("dve", tf[:, 3, :], posi[:, :], [("tf", 4)], [("tf", 3)])
                    tsc("dve", tf[:, 3, :], tf[:, 3, :], invcol, None, ALU.mult, None, [("tf", 3), "cst"], [("tf", 3)])
                    for (shift, dstt, dkey) in ((0.0, Snt, "Snt"), (0.5 * math.pi, Ct, "Ct")):
                        tsc("dve", tf[:, 0, :], tf[:, 3, :], shift, None, ALU.add, None, [("tf", 3)], [("tf", 0)])
                        tsc("dve", tf[:, 1, :], tf[:, 0, :], 1.0 / (2 * math.pi), None, ALU.mult, None, [("tf", 0)], [("tf", 1)])
                        cp("dve", tf[:, 4, :].bitcast(I32), tf[:, 1, :], [("tf", 1)], [("tf", 4)])
                        cp("dve", tf[:, 1, :], tf[:, 4, :].bitcast(I32), [("tf", 4)], [("tf", 1)])
                        stt("dve", tf[:, 0, :], tf[:, 1, :], -2 * math.pi, tf[:, 0, :], ALU.mult, ALU.add, [("tf", 1), ("tf", 0)], [("tf", 0)])
                        tsc("dve", tf[:, 1, :], tf[:, 0, :], math.pi, -2 * math.pi, ALU.is_gt, ALU.mult, [("tf", 0)], [("tf", 1)])
                        tt("dve", tf[:, 0, :], tf[:, 0, :], tf[:, 1, :], ALU.add, [("tf", 0), ("tf", 1)], [("tf", 0)])
                        act(dstt[:, :], tf[:, 0, :], AF.Sin, [("tf", 0)], [dkey])

                    ck(4)
                    xnk = lambda kc: xn[:, kc, :]
                    for h in range(NB):
                        b = proj(l, xnk, KC, ["xn"])
                        act(tf[:, 0, :], pbank[b][:, :], AF.Sigmoid, [("pb", b)], [("tf", 0)])
                        tsc("dve", tf[:, 0, :], tf[:, 0, :], lbc[:, LBW + h:LBW + h + 1], lbc[:, h:h + 1], ALU.mult, ALU.add,
                            [("tf", 0), "lbc"], [("tf", 0)])
                        tr.add("dve", lambda e: e.tensor_tensor_scan(tf[:, 3, :], sflag[:, :], tf[:, 0, :], 1.0, ALU.max, ALU.mult),
                               r=[("tf", 0), "sflag"], w=[("tf", 3)])
                        tsc("dve", tf[:, 0, :], tf[:, 0, :], -1.0, 1.0, ALU.mult, ALU.add, [("tf", 0)], [("tf", 0)])
                        tr.add("dve", lambda e: e.reciprocal(tf[:, 4, :], tf[:, 3, :]), r=[("tf", 3)], w=[("tf", 4)])
                        tt("dve", tf[:, 1, :], tf[:, 0, :], tf[:, 4, :], ALU.mult, [("tf", 0), ("tf", 4)], [("tf", 1)])
                        cp("dve", th[:, 0, :], tf[:, 1, :], [("tf", 1)], [("th", 0)])
                        cp("act", dch[:, :], tf[:, 3, :].rearrange("p (c s) -> p c s", s=64)[:, :, 63], [("tf", 3)], ["dch"])
                        tt("dve", th[:, 1, :].rearrange("p (c s) -> p c s", s=64), tf[:, 1, :].rearrange("p (c s) -> p c s", s=64),
                           dch[:, :].unsqueeze(2).to_broadcast([P, 8, 64]), ALU.mult, [("tf", 1), "dch"], [("th", 1)])
                        for c in range(8):
                            tp(ptr[:64, c * P:(c + 1) * P], th[:, 1, c * 64:(c + 1) * 64], [("th", 1), "cstb"], ["ptr"])
                        cp("act", ktm[:64, :, :], ptr[:64, 0:8 * P].rearrange("p (c k) -> p c k", k=P), ["ptr"], ["ktm"])
                        b = proj(l, xnk, KC, ["xn"])
                        stt("dve", th[:, 2, :], pbank[b][:, :], float(P) ** -0.5, tf[:, 3, :], ALU.mult, ALU.mult,
                            [("pb", b), ("tf", 3)], [("th", 2)])
                        b = proj(l, xnk, KC, ["xn"])
                        cp("act", th[:, 3, :], pbank[b][:, :], [("pb", b)], [("th", 3)])
                        for c in range(8):
                            tp(ptr[:64, c * P:(c + 1) * P], th[:, 3, c * 64:(c + 1) * 64], [("th", 3), "cstb"], ["ptr"])
                        cp("act", vtm[:64, :, :], ptr[:64, 0:8 * P].rearrange("p (c k) -> p c k", k=P), ["ptr"], ["vtm"])
                        b = proj(l, xnk, KC, ["xn"])
                        act(tf[:, 5, :], pbank[b][:, :], AF.Silu, [("pb", b)], [("tf", 5)])
                        for c in range(8):
                            cs = slice(c * 64, (c + 1) * 64)
                            a = c % 2
                            ab = 3 if a == 0 else 6
                            mm(pbank[ab][:64, 0:64], th[:, 0, cs], th[:, 2, cs], True, True,
                               [("th", 0), ("th", 2)], [("pb", ab)])
                            tt("dve", atm[:64, a, :], pbank[ab][:64, 0:64], ucur_f[:64, :64], ALU.mult,
                               [("pb", ab), "cst"], [("atm", a)])
                            mm(pbank[5][:, cs], Sbf[:, h, :], th[:, 2, cs], True, False, [("Sbf", h), ("th", 2)], [("pb", 5)])
                            mm(pbank[5][:, cs], vtm[:64, c, :], atm[:64, a, :], False, True, ["vtm", ("atm", a)], [("pb", 5)])
                            mm(pbank[4][:, 0:P], ktm[:64, c, :], vtm[:64, c, :], True, True, ["ktm", "vtm"], [("pb", 4)])
                            stt("dve", Sst[:, h, :], Sst[:, h, :], dch[:, c:c + 1], pbank[4][:, 0:P], ALU.mult, ALU.add,
                                [("pb", 4), "dch", ("Sst", h)], [("Sst", h)])
                            cp("act", Sbf[:, h, :], Sst[:, h, :], [("Sst", h)], [("Sbf", h)])
                        act(th[:, 4, :], pbank[5][:, :], AF.Square, [("pb", 5)], [("th", 4)])
                        mm(pbank[2][:, :], ones_b, th[:, 4, :], True, True, [("th", 4), "cstb"], [("pb", 2)])
                        rstd_from(("pb", 2), pbank[2][:, :], P, tf[:, 6, :], ("tf", 6))
                        tt("dve", tf[:, 7, :], pbank[5][:, :], tf[:, 6, :], ALU.mult, [("pb", 5), ("tf", 6)], [("tf", 7)])
                        stt("dve", ys[:, h, :], tf[:, 7, :], cols[:, l, O_HG:O_HG + 1], tf[:, 5, :], ALU.mult, ALU.mult,
                            [("tf", 7), ("tf", 5), "cols"], ["ys"])

                        if DEBUG[0] and l == 0 and tb == 0 and h == 0:
                            dma("pool", dbg2[:, 0:P], Sst[:, 0, :], [("Sst", 0)], [], "k_dbg")
                            dma("pool", dbg2[:, P:P + 8], dch[:, :], ["dch"], [], "k_dbg")
                            dma("pool", dbg2[:, 2 * P:2 * P + T], tf[:, 3, :], [("tf", 3)], [], "k_dbg")
                        ck(5)
                        for gi, d in enumerate(DILS):
                            nq = T // d if d > 1 else P
                            ntl = max(d, 4)
                            L = S // d
                            m0 = t0 // d
                            npv = min(P, m0) if d > 1 else (P if tb > 0 else 0)
                            if npv > 0:
                                if d == 1:
                                    dma("pool", akp[:, 0, :], ksc[gi][h * P:(h + 1) * P, t0 - P:t0],
                                        [("ksc", gi, h)], ["akp"], f"k_kp{gi}")
                                    dma("pool", avp[:, 0, :], vsc[gi][h * S + t0 - P:h * S + t0, :],
                                        [("vsc", gi, h)], ["avp"], f"k_vp{gi}")
                                else:
                                    wn = npv * d
                                    dma("pool", kst[:, 0:wn], ksc[gi][h * P:(h + 1) * P, t0 - wn:t0],
                                        [("ksc", gi, h)], ["kst"], "k_kst")
                                    cp("pool", akp[:, 0:d, 0:npv], kst[:, 0:wn].rearrange("e (m r) -> e r m", r=d),
                                       ["kst"], ["akp"])
                                    vv = vsc[gi][h * S + t0 - wn:h * S + t0, :].rearrange("(m r) e -> m r e", r=d)
                                    dma("pool", avp[0:npv, 0:d, :], vv,
                                        [("vsc", gi, h)], ["avp"], f"k_vp{gi}")
                            for which in range(2):
                                b = proj(l, xnk, KC, ["xn"])
                                rg = rgq if which == 0 else rgk
                                gofs = (O_QN if which == 0 else O_KN) + gi
                                act(th[:, 4, :], pbank[b][:, :], AF.Copy, [("pb", b)], [("th", 4)])
                                act(tf[:, 0, :], pbank[b][:, :], AF.Square, [("pb", b)], [("tf", 0)])
                                mm(pbank[2][:, :], ones_f, tf[:, 0, :], True, True, [("tf", 0), "cst"], [("pb", 2)])
                                mm(pbank[3][:, :], rg[:, gi, :], th[:, 4, :], True, True, [("th", 4), "rg"], [("pb", 3)])
                                rstd_from(("pb", 2), pbank[2][:, :], P, tf[:, 1, :], ("tf", 1))
                                stt("dve", tf[:, 2, :], pbank[b][:, :], cols[:, l, gofs:gofs + 1], Ct[:, :], ALU.mult, ALU.mult,
                                    [("pb", b), "cols", "Ct"], [("tf", 2)])
                                tt("dve", tf[:, 3, :], pbank[3][:, :], Snt[:, :], ALU.mult, [("pb", 3), "Snt"], [("tf", 3)])
                                tt("dve", tf[:, 2, :], tf[:, 2, :], tf[:, 3, :], ALU.add, [("tf", 2), ("tf", 3)], [("tf", 2)])
                                dsti = 5 + which
                                if d > 1:
                                    o_ap = th[:, dsti, :].rearrange("e (r m) -> e m r", r=d)
                                    a_ap = tf[:, 2, :].rearrange("e (m r) -> e m r", r=d)
                                    b_ap = tf[:, 1, :].rearrange("e (m r) -> e m r", r=d)
                                else:
                                    o_ap, a_ap, b_ap = th[:, dsti, :], tf[:, 2, :], tf[:, 1, :]
                                if which == 1:
                                    tt("dve", th[:, 0, :], tf[:, 2, :], tf[:, 1, :], ALU.mult, [("tf", 2), ("tf", 1)], [("th", 0)])
                                    if d > 1:
                                        cp("pool", o_ap, th[:, 0, :].rearrange("e (m r) -> e m r", r=d), [("th", 0)], [("th", dsti)])
                                    else:
                                        cp("pool", o_ap, th[:, 0, :], [("th", 0)], [("th", dsti)])
                                else:
                                    tt("dve", o_ap, a_ap, b_ap, ALU.mult, [("tf", 2), ("tf", 1)], [("th", dsti)])
                            dma("pool", ksc[gi][h * P:(h + 1) * P, t0:t0 + T], th[:, 0, :], [("th", 0)], [("ksc", gi, h)], f"k_ks{gi}")
                            b = proj(l, xnk, KC, ["xn"])
                            if d > 1:
                                o_ap = th[:, 7, :].rearrange("e (r m) -> e m r", r=d)
                                a_ap = pbank[b][:, :].rearrange("e (m r) -> e m r", r=d)
                            else:
                                o_ap, a_ap = th[:, 7, :], pbank[b][:, :]
                            cp("act", o_ap, a_ap, [("pb", b)], [("th", 7)])
                            for rnd in range(ntl // 4 if ntl > 8 else 1):
                                lo = rnd * (ntl if ntl <= 8 else 4)
                                cnt = ntl if ntl <= 8 else 4
                                for i in range(cnt):
                                    tl = lo + i
                                    tp(ptr[:nq, i * P:(i + 1) * P], th[:, 7, tl * nq:(tl + 1) * nq], [("th", 7), "cstb"], ["ptr"])
                                cp("act", avt[:nq, lo:lo + cnt, :], ptr[:nq, 0:cnt * P].rearrange("p (c k) -> p c k", k=P),
                                   ["ptr"], ["avt"])
                            if d == 1:
                                vv = vsc[gi][h * S + t0:h * S + t0 + T, :].rearrange("(j p) e -> p j e", p=P)
                                dma("pool", vv, avt[:, 0:4, :], ["avt"], [("vsc", gi, h)], f"k_vs{gi}")
                            else:
                                vv = vsc[gi][h * S + t0:h * S + t0 + T, :].rearrange("(m r) e -> m r e", r=d)
                                dma("pool", vv, avt[:nq, 0:d, :], ["avt"], [("vsc", gi, h)], f"k_vs{gi}")
                            sc = float(P) ** -0.5
                            if gi == 0:
                                ck(69)
                            for tl in range(ntl):
                                qs = slice(tl * nq, (tl + 1) * nq)
                                pbuf = tl % 2
                                if d == 1:
                                    if tl > 0:
                                        kp, vp, npk = th[:, 6, (tl - 1) * P:tl * P], avt[:, tl - 1, :], P
                                        pr = [("th", 6), "avt"]
                                    elif npv > 0:
                                        kp, vp, npk = akp[:, 0, :], avp[:, 0, :], P
                                        pr = ["akp", "avp"]
                                    else:
                                        kp, vp, npk, pr = None, None, 0, []
                                else:
                                    if npv > 0:
                                        kp, vp, npk = akp[:, tl, 0:npv], avp[0:npv, tl, :], npv
                                        pr = ["akp", "avp"]
                                    else:
                                        kp, vp, npk, pr = None, None, 0, []
                                sb_ = 3
                                if npk:
                                    mm(pbank[sb_][:npk, 0:nq], kp, th[:, 5, qs], True, npk != P, pr + [("th", 5)], [("pb", 3)])
                                    if npk == P:
                                        mm(pbank[sb_][:npk, 0:nq], ident_b[:npk, :npk], mbias[:npk, 1, :nq], False, True,
                                           ["cstb", "mbias"], [("pb", 3)])
                                    act(pt_[:npk, pbuf, 0, :nq], pbank[sb_][:npk, 0:nq], AF.Exp, [("pb", 3)], [("pt", pbuf, 0)], scale=sc)
                                mm(pbank[4][:nq, 0:nq], th[:, 6, qs], th[:, 5, qs], True, False, [("th", 6), ("th", 5)], [("pb", 4)])
                                mm(pbank[4][:nq, 0:nq], ident_b[:nq, :nq], mbias[:nq, 0, :nq], False, True, ["cstb", "mbias"], [("pb", 4)])
                                act(pt_[:nq, pbuf, 1, :nq], pbank[4][:nq, 0:nq], AF.Exp, [("pb", 4)], [("pt", pbuf, 1)], scale=sc)
                                if gi == 0 and tl == 0:
                                    ck(83)
                                if npk:
                                    mm(pbank[5][:, qs], vp, pt_[:npk, pbuf, 0, :nq], True, False, pr + [("pt", pbuf, 0)], [("pb", 5)])
                                    mm(pbank[6][:, qs], ones_b[:npk, :], pt_[:npk, pbuf, 0, :nq], True, False, [("pt", pbuf, 0), "cstb"], [("pb", 6)])
                                if gi == 0 and tl == 0:
                                    ck(84)
                                mm(pbank[5][:, qs], avt[:nq, tl, :], pt_[:nq, pbuf, 1, :nq], not npk, True, ["avt", ("pt", pbuf, 1)], [("pb", 5)])
                                mm(pbank[6][:, qs], ones_b[:nq, :], pt_[:nq, pbuf, 1, :nq], not npk, True, [("pt", pbuf, 1), "cstb"], [("pb", 6)])
                                if gi == 0:
                                    ck(70 + tl)
                            if d > 1:
                                o_nat = pbank[5][:, :].rearrange("e (r m) -> e m r", r=d)
                                d_nat = pbank[6][:, :].rearrange("e (r m) -> e m r", r=d)
                                n_ap = tf[:, 8, :].rearrange("e (m r) -> e m r", r=d)
                                dn_ap = tf[:, 9, :].rearrange("e (m r) -> e m r", r=d)
                            else:
                                o_nat, d_nat, n_ap, dn_ap = pbank[5][:, :], pbank[6][:, :], tf[:, 8, :], tf[:, 9, :]
                            ck(50 + gi)
                            if gi == 0:
                                cp("dve", n_ap, o_nat, [("pb", 5)], [("tf", 8)])
                                cp("dve", dn_ap, d_nat, [("pb", 6)], [("tf", 9)])
                            else:
                                tt("dve", n_ap, n_ap, o_nat, ALU.add, [("pb", 5), ("tf", 8)], [("tf", 8)])
                                tt("dve", dn_ap, dn_ap, d_nat, ALU.add, [("pb", 6), ("tf", 9)], [("tf", 9)])
                        tr.add("dve", lambda e: e.reciprocal(tf[:, 9, :], tf[:, 9, :]), r=[("tf", 9)], w=[("tf", 9)])
                        tt("dve", ys[:, NB + h, :], tf[:, 8, :], tf[:, 9, :], ALU.mult, [("tf", 8), ("tf", 9)], ["ys"])

                        ck(6)
                        b = proj(l, xnk, KC, ["xn"])
                        act(uz[:, h, :], pbank[b][:, :], AF.Gelu, [("pb", b)], ["uz"])
                        b = proj(l, xnk, KC, ["xn"])
                        act(vz[:, h, :], pbank[b][:, :], AF.Gelu, [("pb", b)], ["vz"])

                    for h in range(NB):
                        mm(pbank[5][:, :], ones_b, vz[:, h, :], h == 0, h == NB - 1, ["vz", "cstb"], [("pb", 5)])
                    for h in range(NB):
                        s = h % 2
                        act(tf[:, s, :], vz[:, h, :], AF.Square, ["vz"], [("tf", s)])
                        mm(pbank[6][:, :], ones_f, tf[:, s, :], h == 0, h == NB - 1, [("tf", s), "cst"], [("pb", 6)])
                    tsc("dve", tf[:, 2, :], pbank[5][:, :], 1.0 / BW, None, ALU.mult, None, [("pb", 5)], [("tf", 2)])
                    tt("dve", tf[:, 3, :], tf[:, 2, :], tf[:, 2, :], ALU.mult, [("tf", 2)], [("tf", 3)])
                    stt("dve", tf[:, 3, :], pbank[6][:, :], 1.0 / BW, tf[:, 3, :], ALU.mult, ALU.subtract, [("pb", 6), ("tf", 3)], [("tf", 3)])
                    act(tf[:, 3, :], tf[:, 3, :], AF.Sqrt, [("tf", 3), "epsc"], [("tf", 3)], bias=epsc[:, 0:1])
                    tr.add("dve", lambda e: e.reciprocal(tf[:, 3, :], tf[:, 3, :]), r=[("tf", 3)], w=[("tf", 3)])
                    for h in range(NB):
                        tt("dve", tf[:, 4, :], vz[:, h, :], tf[:, 2, :], ALU.subtract, ["vz", ("tf", 2)], [("tf", 4)])
                        tt("dve", tf[:, 4, :], tf[:, 4, :], tf[:, 3, :], ALU.mult, [("tf", 4), ("tf", 3)], [("tf", 4)])
                        tsc("dve", th[:, 0, :], tf[:, 4, :], cols[:, l, O_LG + h:O_LG + h + 1], cols[:, l, O_LB + h:O_LB + h + 1],
                            ALU.mult, ALU.add, [("tf", 4), "cols"], [("th", 0)])
                        for c in range(4):
                            tp(ptr[:, c * P:(c + 1) * P], th[:, 0, c * P:(c + 1) * P], [("th", 0), "cstb"], ["ptr"])
                        cp("act", th[:, 1, :], ptr[:, 0:T], ["ptr"], [("th", 1)])
                        for c in range(4):
                            mm(pbank[4][:, c * P:(c + 1) * P], th[:, 1, c * P:(c + 1) * P], sgw[:, h * P:(h + 1) * P], True, True,
                               [("th", 1), "sgw"], [("pb", 4)])
                        tt("dve", tf[:, 5, :].rearrange("p (c t) -> p c t", t=P), pbank[4][:, :].rearrange("p (c t) -> p c t", t=P),
                           sgb[:, h * P:(h + 1) * P].unsqueeze(1).to_broadcast([P, 4, P]), ALU.add, [("pb", 4), "sgb"], [("tf", 5)])
                        tt("dve", ys[:, 2 * NB + h, :], tf[:, 5, :], uz[:, h, :], ALU.mult, [("tf", 5), "uz"], ["ys"])

                    if DEBUG[0] and l == 0:
                        dma("pool", dbg.rearrange("(c p) t -> p c t", p=P)[:, :, t0:t0 + T], ys[:, :, :], ["ys"], [], "k_dbg")
                    ck(7)
                    for j in range(KC):
                        bslot = wload(l)
                        for i in range(3):
                            bb = 4 + i
                            for kc in range(NB):
                                o = (i * NB + kc) * P
                                mm(pbank[bb][:, :], wsl[:, bslot, o:o + P], ys[:, i * NB + kc, :], kc == 0, kc == NB - 1,
                                   [("wsl", bslot), "ys"], [("pb", bb)])
                        for i in range(3):
                            b = proj(l, xnk, KC, ["xn"])
                            act(tf[:, i, :], pbank[b][:, :], AF.Sigmoid, [("pb", b)], [("tf", i)])
                        for i in range(3):
                            bb = 4 + i
                            if i == 0:
                                tt("dve", tf[:, 3, :], tf[:, 0, :], pbank[bb][:, :], ALU.mult, [("tf", 0), ("pb", bb)], [("tf", 3)])
                            else:
                                tt("dve", tf[:, 4, :], tf[:, i, :], pbank[bb][:, :], ALU.mult, [("tf", i), ("pb", bb)], [("tf", 4)])
                                if i == 1:
                                    tt("dve", tf[:, 3, :], tf[:, 3, :], tf[:, 4, :], ALU.add, [("tf", 3), ("tf", 4)], [("tf", 3)])
                                else:
                                    tt("dve", mh[:, j, :], tf[:, 3, :], tf[:, 4, :], ALU.add, [("tf", 3), ("tf", 4)], ["mh"])
                    ck(8)
                    tr.barrier()
                    for k0 in range(0, KC, 4):
                        dma("pool", xres[:, k0:k0 + 4, :], src.rearrange("(kc p) t -> p kc t", p=P)[:, k0:k0 + 4, t0:t0 + T],
                            [("xd", l)], ["xres"], "k_x")
                    if l + 1 < NL and pending_cast[l + 1]:
                        per = (len(cast_chunks(l + 1)) + NTB - 2) // max(1, NTB - 1)
                        for _ in range(per if tb < NTB - 1 else len(pending_cast[l + 1])):
                            if pending_cast[l + 1]:
                                issue_cast(pending_cast[l + 1].pop(0))
                    for j in range(KC):
                        b = proj(l, lambda kc: mh[:, kc, :], KC, ["mh"])
                        tt("dve", xres[:, j, :], xres[:, j, :], pbank[b][:, :], ALU.add, [("pb", b), "xres"], ["xres"])
                    ck(9)
                    norm_block(l, O_GF)
                    for pt in range(3):
                        for jb in range(KC):
                            fb = pt * KC + jb
                            bg = proj(l, xnk, KC, ["xn"])
                            cp(MASKENG, gext[:, 0:2], carry[:, fb, :], ["carry"], ["gext"])
                            cp("act", gext[:, 2:T + 2], pbank[bg][:, :], [("pb", bg)], ["gext"])
                            bv = proj(l, xnk, KC, ["xn"])
                            cwo = O_CW + fb
                            tsc("dve", tf[:, 0, :], gext[:, 2:T + 2], cols[:, l, cwo + 2 * FC:cwo + 2 * FC + 1],
                                cols[:, l, O_CB + fb:O_CB + fb + 1], ALU.mult, ALU.add, ["gext", "cols"], [("tf", 0)])
                            stt("dve", tf[:, 0, :], gext[:, 1:T + 1], cols[:, l, cwo + FC:cwo + FC + 1], tf[:, 0, :],
                                ALU.mult, ALU.add, ["gext", "cols", ("tf", 0)], [("tf", 0)])
                            stt("dve", tf[:, 0, :], gext[:, 0:T], cols[:, l, cwo:cwo + 1], tf[:, 0, :],
                                ALU.mult, ALU.add, ["gext", "cols", ("tf", 0)], [("tf", 0)])
                            cp(MASKENG, carry[:, fb, :], gext[:, T:T + 2], ["gext"], ["carry"])
                            act(tf[:, 1, :], tf[:, 0, :], AF.Silu, [("tf", 0)], [("tf", 1)])
                            tt("dve", mh[:, jb, :], tf[:, 1, :], pbank[bv][:, :], ALU.mult, [("tf", 1), ("pb", bv)], ["mh"])
                        for j in range(KC):
                            b = proj(l, lambda kc: mh[:, kc, :], KC, ["mh"])
                            tt("dve", xres[:, j, :], xres[:, j, :], pbank[b][:, :], ALU.add, [("pb", b), "xres"], ["xres"])
                    ck(10)
                    for k0 in range(0, KC, 4):
                        dma("pool", dst.rearrange("(kc p) t -> p kc t", p=P)[:, k0:k0 + 4, t0:t0 + T], xres[:, k0:k0 + 4, :],
                            ["xres"], [("xd", l + 1)], "k_xo")
                    ck(11)
                    if tb == 1:
                        ck(12)
                    if tb == NTB - 1:
                        ck(13)
        except _Stop:
            pass

        tr.finalize()
        tr.simulate()
        if DEBUG[0] >= 2:
            tr.dump_tail()
        keys = set(Tracker.ENG) | set(tr.dma_cnt.keys())
        sems = {k: es.enter_context(nc.semaphore(f"s_{k}")) for k in sorted(keys)}
        with nc.Block() as block:
            @block.tensor
            def _(e):
                tr.emit("pe", e, sems)

            @block.scalar
            def _(e):
                tr.emit("act", e, sems)

            @block.vector
            def _(e):
                tr.emit("dve", e, sems)

            @block.gpsimd
            def _(e):
                tr.emit("pool", e, sems)

            @block.sync
            def _(e):
                tr.emit("sp", e, sems)
    return nc


def make_inputs(D, S, NL, b, x, positions, norm_mix, w_in, hgrn_lower_bounds, hgrn_out_norm, q_norm, k_norm,
                sg_ln_g, sg_ln_b, sg_w, sg_b, w_gate, w_branch, w_out, norm_ffn, w_up, ffn_conv_w, ffn_conv_b, w_down,
                shared=None):
    KC, BW, NB, DFF, FC, NTB, NT = _dims(D, S)
    if shared is None:
        shared = {}
        shared["consts"] = make_consts_full()
        shared["cols"] = np.concatenate([pack_cols(D, S, l, norm_mix, hgrn_out_norm, q_norm, k_norm, sg_ln_g, sg_ln_b,
                                                   norm_ffn, ffn_conv_w, ffn_conv_b) for l in range(NL)], axis=0)
        LBW = max(NB, 8)
        lbraw = np.zeros((P, NL, LBW), np.float32)
        for l in range(NL):
            lbraw[:, l, :NB] = hgrn_lower_bounds[l].reshape(NB, P).T
        shared["lbraw"] = lbraw.reshape(P, NL * LBW)
        shared["sgb"] = np.ascontiguousarray(np.broadcast_to(sg_b.reshape(NL, 1, NB * P), (NL, P, NB * P)).reshape(NL * P, NB * P).astype(np.float32))
        shared["sgwT"] = np.ascontiguousarray(sg_w.transpose(0, 3, 1, 2).reshape(NL * P, NB * P).astype(np.float32))
        shared["wall"] = np.concatenate([pack_weights(D, S, l, w_in, w_gate, w_branch, w_out, w_up, w_down)
                                         for l in range(NL)], axis=0).reshape(NL * NT * P, KC * P)
    m = dict(shared)
    m["xT"] = np.ascontiguousarray(x[b].T)
    m["pos"] = np.ascontiguousarray(np.broadcast_to(positions[b].reshape(1, S), (P, S)).astype(np.int32))
    return m, shared


def run(D, S, NL, inputs):
    inputs = {k: np.asarray(v) for k, v in inputs.items()}
    B = inputs["x"].shape[0]
    nc = build(D, S, NL)
    in_maps = []
    shared = None
    for b in range(B):
        m, shared = make_inputs(D, S, NL, b, shared=shared, **inputs)
        in_maps.append(m)
    res = run_bass_kernel_spmd(nc, in_maps, core_ids=list(range(B)))
    LAST["res"] = res.results
    out = np.stack([np.ascontiguousarray(res.results[b]["outT"].T) for b in range(B)], axis=0)
    return out.astype(np.float32)


def kernel(**inputs):
    return run(4096, 4096, 2, inputs)
```
